# Optimizing a Trainium2 kernel written in Bass

```python
import math
import jax, jax.numpy as jnp
from jax import lax
import numpy as np

D_MODEL = 2048
BATCH = 4
SEQ = 4096
DEPTH = 2

N_META = 16
BLOCK = 128
WINDOW = 128
META_PAD = BLOCK - N_META
HEAD_DIM = 64
N_Q_HEADS = 16
N_KV_HEADS = 4
Q_PER_KV = N_Q_HEADS // N_KV_HEADS
ATTN_WIDTH = N_Q_HEADS * HEAD_DIM
KV_WIDTH = N_KV_HEADS * HEAD_DIM
SSM_WIDTH = D_MODEL // 2
SSM_GROUP = 16
SSM_GROUPS = SSM_WIDTH // SSM_GROUP
SSM_STATE = 64
DT_MIN = 1e-3
DT_MAX = 1e-1
D_FF = 4 * D_MODEL
N_BUCKETS = 32
MAX_DISTANCE = 128
DN_ALPHA = (2 * DEPTH) ** 0.25
DN_BETA = (8 * DEPTH) ** -0.25
LN_EPS = 1e-5
NEG_INF = -1e30
IN_WIDTH = ATTN_WIDTH + 2 * KV_WIDTH + SSM_WIDTH + 2 * D_MODEL
SPLITS = [ATTN_WIDTH, ATTN_WIDTH + KV_WIDTH, ATTN_WIDTH + 2 * KV_WIDTH,
          ATTN_WIDTH + 2 * KV_WIDTH + SSM_WIDTH]

kernel_name = 'hybrid_s5_swa_gated_deepnorm'


def layer_norm(x, g, b):
    xf = x.astype(jnp.float32)
    mu = jnp.mean(xf, axis=-1, keepdims=True)
    xc = xf - mu
    var = jnp.mean(xc * xc, axis=-1, keepdims=True)
    y = xc * lax.rsqrt(var + LN_EPS) * g.astype(jnp.float32) + b.astype(jnp.float32)
    return y.astype(x.dtype)


def t5_bucket(dist):
    n = jnp.maximum(dist, 0)
    max_exact = N_BUCKETS // 2
    nf = jnp.maximum(n, 1).astype(jnp.float32)
    large = max_exact + (jnp.log(nf / max_exact) / math.log(MAX_DISTANCE / max_exact)
                         * (N_BUCKETS - max_exact)).astype(jnp.int32)
    large = jnp.minimum(large, N_BUCKETS - 1)
    return jnp.where(n < max_exact, n, large)


def band_layout(n_blocks):
    blk = jnp.arange(n_blocks, dtype=jnp.int32)[:, None]
    j = jnp.arange(BLOCK, dtype=jnp.int32)[None, :]
    q_pos = blk * BLOCK + j
    k_meta = jnp.broadcast_to(j, (n_blocks, BLOCK))
    k_prev = (blk - 1) * BLOCK + j
    k_pos = jnp.concatenate([k_meta, k_prev, q_pos], axis=1)
    dist = q_pos[:, :, None] - k_pos[:, None, :]
    kp = k_pos[:, None, :]
    is_meta_seg = (jnp.arange(3 * BLOCK) < BLOCK)[None, None, :]
    meta_ok = kp >= META_PAD
    real_ok = (kp >= BLOCK) & (dist < WINDOW)
    valid = (dist >= 0) & jnp.where(is_meta_seg, meta_ok, real_ok)
    return dist, valid


def sliding_window_attention(q, k, v, attn_bias, sinks):
    b, lp, _ = q.shape
    nb = lp // BLOCK
    q = q.reshape(b, nb, BLOCK, N_KV_HEADS, Q_PER_KV, HEAD_DIM)

    def bands(t):
        t = t.reshape(b, nb, BLOCK, N_KV_HEADS, HEAD_DIM)
        meta = jnp.broadcast_to(t[:, :1], t.shape)
        prev = jnp.pad(t[:, :-1], ((0, 0), (1, 0), (0, 0), (0, 0), (0, 0)))
        return jnp.concatenate([meta, prev, t], axis=2)

    kb, vb = bands(k), bands(v)
    s = jnp.einsum('bnqkgd,bnskd->bnkgqs', q, kb).astype(jnp.float32) * (HEAD_DIM ** -0.5) + attn_bias
    sink = sinks.astype(jnp.float32).reshape(1, 1, N_KV_HEADS, Q_PER_KV, 1, 1)
    m = jnp.maximum(jnp.max(s, axis=-1, keepdims=True), sink)
    p = jnp.exp(s - m)
    p = p / (jnp.sum(p, axis=-1, keepdims=True) + jnp.exp(sink - m))
    o = jnp.einsum('bnkgqs,bnskd->bnqkgd', p.astype(vb.dtype), vb)
    return o.reshape(b, lp, ATTN_WIDTH)


def s5_mixer(u, lam_re, lam_im, log_step, b_re, b_im, c_re, c_im, d_skip, w_glu):
    bsz, L, _ = u.shape
    uf = u.astype(jnp.float32).reshape(bsz, L, SSM_GROUPS, SSM_GROUP)
    lr, li = lam_re.astype(jnp.float32), lam_im.astype(jnp.float32)
    dt = jnp.exp(log_step.astype(jnp.float32))[:, None]
    decay = jnp.exp(lr * dt)
    ar, ai = decay * jnp.cos(li * dt), decay * jnp.sin(li * dt)
    den = lr * lr + li * li
    nr, ni = ar - 1.0, ai
    zr = (nr * lr + ni * li) / den
    zi = (ni * lr - nr * li) / den
    br_, bi_ = b_re.astype(jnp.float32), b_im.astype(jnp.float32)
    bbar_re = zr[..., None] * br_ - zi[..., None] * bi_
    bbar_im = zr[..., None] * bi_ + zi[..., None] * br_
    xr = jnp.einsum('blgp,gnp->blgn', uf, bbar_re)
    xi = jnp.einsum('blgp,gnp->blgn', uf, bbar_im)
    a_r = jnp.broadcast_to(ar, (1, L) + ar.shape)
    a_i = jnp.broadcast_to(ai, (1, L) + ai.shape)

    def combine(e1, e2):
        a1r, a1i, b1r, b1i = e1
        a2r, a2i, b2r, b2i = e2
        return (a1r * a2r - a1i * a2i,
                a1r * a2i + a1i * a2r,
                a2r * b1r - a2i * b1i + b2r,
                a2r * b1i + a2i * b1r + b2i)

    _, _, hr, hi = lax.associative_scan(combine, (a_r, a_i, xr, xi), axis=1)
    y = (jnp.einsum('blgn,gpn->blgp', hr, c_re.astype(jnp.float32))
         - jnp.einsum('blgn,gpn->blgp', hi, c_im.astype(jnp.float32))
         + d_skip.astype(jnp.float32) * uf)
    y = jax.nn.gelu(y.reshape(bsz, L, SSM_WIDTH))
    y = y * jax.nn.sigmoid(y @ w_glu.astype(jnp.float32))
    return y.astype(u.dtype)


def hybrid_layer(h, attn_bias, in_proj, gate_b, sinks, lam_re, lam_im, log_step, b_re, b_im,
                 c_re, c_im, d_skip, w_glu, w_attn_up, w_ssm_up, w_out, ln_mix_g, ln_mix_b,
                 w_mlp_up, w_mlp_down, ln_mlp_g, ln_mlp_b):
    z = h @ in_proj
    q, k, v, u, g = jnp.split(z, SPLITS, axis=-1)
    gates = jax.nn.sigmoid((g + gate_b).astype(jnp.float32)).astype(h.dtype)
    g_attn, g_ssm = jnp.split(gates, 2, axis=-1)
    pad = ((0, 0), (META_PAD, 0), (0, 0))
    y_attn = sliding_window_attention(jnp.pad(q, pad), jnp.pad(k, pad), jnp.pad(v, pad),
                                      attn_bias, sinks)[:, META_PAD:]
    y_ssm = s5_mixer(u, lam_re, lam_im, log_step, b_re, b_im, c_re, c_im, d_skip, w_glu)
    mixed = (g_attn * (y_attn @ w_attn_up) + g_ssm * (y_ssm @ w_ssm_up)) @ w_out
    h = layer_norm(DN_ALPHA * h + mixed, ln_mix_g, ln_mix_b)
    f = jnp.square(jax.nn.relu(h @ w_mlp_up)) @ w_mlp_down
    return layer_norm(DN_ALPHA * h + f, ln_mlp_g, ln_mlp_b)


def setup_inputs(seed: int = 0) -> dict:
    key = jax.random.key(seed)
    ks = jax.random.split(key, 32)
    f32 = jnp.float32

    def nrm(k, shape, scale):
        return jax.random.normal(k, shape, f32) * scale

    G, N, P = SSM_GROUPS, SSM_STATE, SSM_GROUP
    n_idx = jnp.arange(N, dtype=f32)
    return {
        'x': nrm(ks[0], (BATCH, SEQ, D_MODEL), 1.0),
        'meta_tokens': nrm(ks[1], (N_META, D_MODEL), 1.0),
        'ln_emb_g': 1.0 + nrm(ks[2], (D_MODEL,), 0.02),
        'ln_emb_b': nrm(ks[3], (D_MODEL,), 0.02),
        'rel_bias': nrm(ks[4], (N_BUCKETS, N_Q_HEADS), 0.5),
        'in_proj': nrm(ks[5], (DEPTH, D_MODEL, IN_WIDTH), D_MODEL ** -0.5),
        'gate_b': nrm(ks[6], (DEPTH, 2 * D_MODEL), 0.02),
        'attn_sinks': nrm(ks[7], (DEPTH, N_Q_HEADS), 0.5),
        'ssm_lambda_re': -0.5 + nrm(ks[8], (DEPTH, G, N), 0.01),
        'ssm_lambda_im': math.pi * n_idx + nrm(ks[9], (DEPTH, G, N), 0.01),
        'ssm_log_step': jax.random.uniform(ks[10], (DEPTH, G), f32, math.log(DT_MIN), math.log(DT_MAX)),
        'ssm_b_re': nrm(ks[11], (DEPTH, G, N, P), (2 * P) ** -0.5),
        'ssm_b_im': nrm(ks[12], (DEPTH, G, N, P), (2 * P) ** -0.5),
        'ssm_c_re': nrm(ks[13], (DEPTH, G, P, N), 0.5 ** 0.5),
        'ssm_c_im': nrm(ks[14], (DEPTH, G, P, N), 0.5 ** 0.5),
        'ssm_d': nrm(ks[15], (DEPTH, G, P), 0.5),
        'ssm_w_glu': nrm(ks[16], (DEPTH, SSM_WIDTH, SSM_WIDTH), SSM_WIDTH ** -0.5),
        'w_attn_up': nrm(ks[17], (DEPTH, ATTN_WIDTH, D_MODEL), ATTN_WIDTH ** -0.5),
        'w_ssm_up': nrm(ks[18], (DEPTH, SSM_WIDTH, D_MODEL), SSM_WIDTH ** -0.5),
        'w_out': nrm(ks[19], (DEPTH, D_MODEL, D_MODEL), DN_BETA * D_MODEL ** -0.5),
        'ln_mix_g': 1.0 + nrm(ks[20], (DEPTH, D_MODEL), 0.02),
        'ln_mix_b': nrm(ks[21], (DEPTH, D_MODEL), 0.02),
        'w_mlp_up': nrm(ks[22], (DEPTH, D_MODEL, D_FF), D_MODEL ** -0.5),
        'w_mlp_down': nrm(ks[23], (DEPTH, D_FF, D_MODEL), DN_BETA * D_FF ** -0.5),
        'ln_mlp_g': 1.0 + nrm(ks[24], (DEPTH, D_MODEL), 0.02),
        'ln_mlp_b': nrm(ks[25], (DEPTH, D_MODEL), 0.02),
    }


def reference(x, meta_tokens, ln_emb_g, ln_emb_b, rel_bias, in_proj, gate_b, attn_sinks,
              ssm_lambda_re, ssm_lambda_im, ssm_log_step, ssm_b_re, ssm_b_im, ssm_c_re, ssm_c_im,
              ssm_d, ssm_w_glu, w_attn_up, w_ssm_up, w_out, ln_mix_g, ln_mix_b,
              w_mlp_up, w_mlp_down, ln_mlp_g, ln_mlp_b):
    bsz, seq, _ = x.shape
    meta = jnp.broadcast_to(meta_tokens[None].astype(x.dtype), (bsz, N_META, D_MODEL))
    h = layer_norm(jnp.concatenate([meta, x], axis=1), ln_emb_g, ln_emb_b)

    n_blocks = (seq + BLOCK) // BLOCK
    dist, valid = band_layout(n_blocks)
    bias = rel_bias.astype(jnp.float32)[t5_bucket(dist)]
    bias = bias.transpose(0, 3, 1, 2).reshape(n_blocks, N_KV_HEADS, Q_PER_KV, BLOCK, 3 * BLOCK)
    attn_bias = jnp.where(valid[:, None, None], bias, NEG_INF)

    for l in range(DEPTH):
        h = hybrid_layer(h, attn_bias, in_proj[l], gate_b[l], attn_sinks[l],
                         ssm_lambda_re[l], ssm_lambda_im[l], ssm_log_step[l],
                         ssm_b_re[l], ssm_b_im[l], ssm_c_re[l], ssm_c_im[l], ssm_d[l],
                         ssm_w_glu[l], w_attn_up[l], w_ssm_up[l], w_out[l],
                         ln_mix_g[l], ln_mix_b[l], w_mlp_up[l], w_mlp_down[l],
                         ln_mlp_g[l], ln_mlp_b[l])
    return h[:, N_META:]
```

```python
import contextlib
import math
import numpy as np
import concourse.bass as bass
import concourse.mybir as mybir
from concourse.bass_utils import run_bass_kernel_spmd

F32 = mybir.dt.float32
BF16 = mybir.dt.bfloat16
I32 = mybir.dt.int32
AF = mybir.ActivationFunctionType
ALU = mybir.AluOpType

D = 2048
KC = 16
SEQ = 4096
NMETA = 16
NT = 512
DEPTH = 2
INW = 6656
DFF = 8192
ALPHA = (2 * DEPTH) ** 0.25
EPS = 1e-5
EXPS = [1, 2, 3, 4, 5, 6, 7, 8, 16, 32, 64, 128, 256, 512]
EIDX = {e: i for i, e in enumerate(EXPS)}
NE = len(EXPS)
VARIANT = 0
ENGS = ["tensor", "vector", "scalar", "gpsimd", "sync"]


class _Rec:
    def __init__(self):
        self.calls = []

    def __getattr__(self, name):
        def f(*a, **k):
            self.calls.append((name, a, k))
            return self
        return f


class Sched:
    NDMA = 6

    def __init__(self, nc):
        self.nc = nc
        self.ops = {e: [] for e in ENGS}
        self.last_write = {}
        self.readers = {}

    def _add(self, eng, fn, reads, writes, dma):
        idx = len(self.ops[eng])
        deps = set()
        for k in reads:
            lw = self.last_write.get(k)
            if lw is not None:
                deps.add(lw)
        for k in writes:
            lw = self.last_write.get(k)
            if lw is not None:
                deps.add(lw)
            for r in self.readers.get(k, ()):
                deps.add(r)
        me = (eng, idx)
        deps.discard(me)
        rec = _Rec()
        fn(rec)
        assert len(rec.calls) == 1, rec.calls
        self.ops[eng].append(dict(call=rec.calls[0], deps=deps, sig=False, dma=dma))
        for k in writes:
            self.last_write[k] = me
            self.readers[k] = []
        for k in reads:
            if k not in writes:
                self.readers.setdefault(k, []).append(me)
        return me

    def op(self, eng, fn, reads=(), writes=()):
        return self._add(eng, fn, tuple(reads), tuple(writes), False)

    def dma(self, eng, fn, reads=(), writes=()):
        return self._add(eng, fn, tuple(reads), tuple(writes), True)

    def emit(self):
        nc = self.nc
        ops = self.ops
        for e in ENGS:
            for o in ops[e]:
                for (de, di) in o["deps"]:
                    d = ops[de][di]
                    if not d["dma"] and not (de == "tensor" and e == "tensor"):
                        d["sig"] = True
        for e in ENGS:
            c = 0
            nd = 0
            for o in ops[e]:
                if o["dma"]:
                    o["dslot"] = nd % self.NDMA
                    o["dtarget"] = 16 * (nd // self.NDMA + 1)
                    o["dn"] = nd
                    nd += 1
                elif o["sig"]:
                    c += 1
                    o["cnt"] = c
        with contextlib.ExitStack() as st:
            csem = {e: st.enter_context(nc.semaphore("c_" + e)) for e in ENGS}
            dsem = {e: [st.enter_context(nc.semaphore("d_%s%d" % (e, i))) for i in range(self.NDMA)]
                    for e in ("sync", "scalar", "gpsimd")}
            block = st.enter_context(nc.Block())

            def run(e, eng):
                waited = {}

                def wait(key, sem, val):
                    if waited.get(key, 0) >= val:
                        return
                    eng.wait_ge(sem, val)
                    waited[key] = val

                dma_list = [o for o in ops[e] if o["dma"]]
                for o in ops[e]:
                    for (de, di) in sorted(o["deps"]):
                        d = ops[de][di]
                        if d["dma"]:
                            wait(("d", de, d["dslot"]), dsem[de][d["dslot"]], d["dtarget"])
                        elif not (de == "tensor" and e == "tensor"):
                            wait(("c", de), csem[de], d["cnt"])
                    if o["dma"]:
                        if o["dn"] >= self.NDMA:
                            p = dma_list[o["dn"] - self.NDMA]
                            wait(("d", e, p["dslot"]), dsem[e][p["dslot"]], p["dtarget"])
                        nm, a, k = o["call"]
                        ins = getattr(eng, nm)(*a, **k)
                        ins.then_inc(dsem[e][o["dslot"]], 16)
                    else:
                        nm, a, k = o["call"]
                        ins = getattr(eng, nm)(*a, **k)
                        if o["sig"]:
                            ins.then_inc(csem[e], 1)
                for p in dma_list[-self.NDMA:]:
                    wait(("d", e, p["dslot"]), dsem[e][p["dslot"]], p["dtarget"])

            block.tensor(lambda eng: run("tensor", eng))
            block.vector(lambda eng: run("vector", eng))
            block.scalar(lambda eng: run("scalar", eng))
            block.gpsimd(lambda eng: run("gpsimd", eng))
            block.sync(lambda eng: run("sync", eng))


def _t5_bucket(n):
    n = np.maximum(n, 0)
    nf = np.maximum(n, 1).astype(np.float32)
    large = 16 + (np.log(nf / np.float32(16)) / np.float32(math.log(8.0)) * np.float32(16)).astype(np.int32)
    large = np.minimum(large, 31)
    return np.where(n < 16, n, large)


def _oh_table(dists, valid):
    t = np.zeros((33, len(dists)), np.float32)
    b = _t5_bucket(dists)
    for j in range(len(dists)):
        if valid[j]:
            t[b[j], j] = 1.0
        else:
            t[32, j] = 1.0
    return t


def _host_consts():
    j = np.arange(255)
    d_cur = 127 - j
    oh_cur = _oh_table(d_cur, d_cur >= 0)
    d_prev = 255 - j
    oh_prev = _oh_table(d_prev, d_prev < 128)
    j = np.arange(143)
    oh_m1 = _oh_table(143 - j, np.ones(143, bool))
    j = np.arange(31)
    oh_m0 = _oh_table(15 - j, (15 - j) >= 0)
    oh_mc = np.zeros((33, 16), np.float32)
    oh_mc[31, :] = 1.0
    oh = np.zeros((33, 255 + 255 + 143 + 31 + 16), np.float32)
    oh[:, 0:255] = oh_cur
    oh[:, 255:510] = oh_prev
    oh[:, 510:653] = oh_m1
    oh[:, 653:684] = oh_m0
    oh[:, 684:700] = oh_mc
    p = np.arange(128)
    g2 = (p % 32) // 16
    g2mask = np.stack([(g2 == 0), (g2 == 1)], axis=1).astype(np.float32)
    return oh, g2mask


OH_CUR, OH_PREV, OH_M1, OH_M0, OH_MC = 0, 255, 510, 653, 684


class _Stop(Exception):
    pass


def build_program(n_tiles=9, n_layers=2, debug=False, stop_after=0):
    nc = bass.Bass("TRN2", target_bir_lowering=False)

    def din(name, shape):
        return nc.dram_tensor(name, list(shape), F32, kind="ExternalInput").ap()

    xT = din("xT", [D, SEQ])
    metaT = din("metaT", [D, NMETA])
    lnp_d = din("lnp", [128, 10, 16])
    gateb_d = din("gateb", [128, 2, 32])
    sinks_d = din("sinks", [64, 2, 16])
    relb_d = din("relb", [33, 16])
    oh_d = din("oh", [33, 700])
    g2m_d = din("g2mask", [128, 2])
    lam_d = din("lam", [128, 2, 3, 32])
    bmat_d = din("bmat", [128, 2, 2, 32, 16])
    cmat_d = din("cmat", [128, 2, 2, 8, 64])
    dvec_d = din("dvec", [128, 2, 8])
    w_in = din("in_proj", [2, D, INW])
    w_glu = din("ssm_w_glu", [2, 1024, 1024])
    w_au = din("w_attn_up", [2, 1024, D])
    w_su = din("w_ssm_up", [2, 1024, D])
    w_o = din("w_out", [2, D, D])
    w_up = din("w_mlp_up", [2, D, DFF])
    w_dn = din("w_mlp_down", [2, DFF, D])
    outT = nc.dram_tensor("outT", [D, SEQ], F32, kind="ExternalOutput").ap()
    dbg = None
    if debug:
        dbg = nc.dram_tensor("dbg", [128, 8, 16, 528], F32, kind="ExternalOutput").ap()

    S = Sched(nc)
    with contextlib.ExitStack() as st:
        def sb(name, shape, dt=F32):
            return st.enter_context(nc.sbuf_tensor(name, list(shape), dt))

        psall = st.enter_context(nc.psum_tensor("psall", [128, 8, 512], F32))
        PS = [psall[:, i, :] for i in range(8)]
        PK = ["PS%d" % i for i in range(8)]

        h = sb("h", [128, KC, NT])
        hb = sb("hb", [128, KC, NT], BF16)
        slabs = [sb("slab%d" % i, [128, 8192], BF16) for i in range(2)]
        ident = sb("ident", [128, 128])
        ones = sb("ones", [128, 128])
        ones_bf = sb("ones_bf", [128, 128], BF16)
        halfpi = sb("halfpi", [128, 1])
        lnp = sb("lnp_sb", [128, 10, 16])
        gateb = sb("gateb_sb", [128, 2, 32])
        g2m = sb("g2m_sb", [128, 2])
        dvec = sb("dvec_sb", [128, 2, 8])
        relb = sb("relb_sb", [33, 16])
        E_cur = sb("E_cur", [128, 16, 128])
        E_prev = sb("E_prev", [128, 16, 128])
        E_m1 = sb("E_m1", [128, 16, 128])
        E_m0 = sb("E_m0", [128, 16, 16])
        E_mc = sb("E_mc", [128, 16])
        sinkE = sb("sinkE", [128, 2, 16])
        apr = sb("apr", [128, 2, 32, NE])
        api = sb("api", [128, 2, 32, NE])
        TB = sb("TB", [128, 2, 8, 2, 128], BF16)
        CE = sb("CE", [128, 2, 8, 2, 128], BF16)
        carry_re = sb("carry_re", [128, 2, 32])
        carry_im = sb("carry_im", [128, 2, 32])
        prevK = sb("prevK", [128, 2, 4, 128], BF16)
        prevV = sb("prevV", [128, 2, 4, 128], BF16)
        metaK = sb("metaK", [128, 2, 4, 128], BF16)
        metaV = sb("metaV", [128, 2, 4, 128], BF16)
        UN = sb("union", [128, 16384])

        def uview(off_b, shape, dt):
            n = int(np.prod(shape[1:]))
            esz = 4 if dt == F32 or dt == I32 else 2
            assert off_b % 4 == 0 and off_b + n * esz <= 65536, (off_b, shape)
            if dt == F32:
                v = UN[:shape[0], off_b // 4: off_b // 4 + n]
            else:
                v = UN[:shape[0], off_b // 4: off_b // 4 + (n * esz + 3) // 4].bitcast(dt)[:, 0:n]
            if len(shape) == 2:
                return v
            names = " ".join("d%d" % i for i in range(1, len(shape)))
            kw = {"d%d" % i: shape[i] for i in range(2, len(shape))}
            return v.rearrange("p (%s) -> p %s" % (names, names), **kw)

        KB = 1024
        scr = sb("scr", [128, 8])

        def barrier(old, new):
            S.op("vector", lambda e: e.memset(scr[:, 0:1], 0.0), reads=[], writes=list(old) + list(new))

        LNK = ["lnsq0", "lnsq1", "lnmean", "lnrstd", "lnt0", "lnt1"]
        SSMK = ["u_bf", "yg", "Xre", "Xim", "Hre", "Him", "tt0", "tt1", "tt2", "tt3", "SA0r", "SA0i", "SA1r", "SA1i", "ytmp0", "ytmp1"]
        ATK = ["qT", "kT2", "vt", "yat"]
        ATT = ["ex%d" % i for i in range(6)] + ["pT%d" % i for i in range(6)] + ["rl0", "rl1"]
        GK = ["mixed", "gt0", "gt1", "gt2", "gt3"]
        slab_ctr = [0]

        def load_slab(parts):
            i = slab_ctr[0] % 2
            slab_ctr[0] += 1
            sl = slabs[i]
            key = "slab%d" % i
            for dstf, src in parts:
                S.dma("gpsimd", lambda e, dstf=dstf, src=src, sl=sl: e.dma_start(out=dstf(sl), in_=src), writes=[key])
            return sl, key

        def v3(sl, off, k, n, parts=128):
            return sl[:parts, off:off + k * n].rearrange("p (k n) -> p k n", n=n)

        ps_ctr = [0]

        def next_ps(lo=0, hi=4):
            i = lo + ps_ctr[0] % (hi - lo)
            ps_ctr[0] += 1
            return PS[i], PK[i]

        S.dma("sync", lambda e: e.dma_start(out=lnp[:], in_=lnp_d), writes=["lnp"])
        S.dma("sync", lambda e: e.dma_start(out=gateb[:], in_=gateb_d), writes=["gateb"])
        S.dma("sync", lambda e: e.dma_start(out=g2m[:], in_=g2m_d), writes=["g2m"])
        S.dma("sync", lambda e: e.dma_start(out=dvec[:], in_=dvec_d), writes=["dvec"])
        S.dma("sync", lambda e: e.dma_start(out=relb[:], in_=relb_d), writes=["relb"])
        S.dma("sync", lambda e: e.dma_start(out=sinkE[0:64], in_=sinks_d), writes=["sinkE"])
        S.op("gpsimd", lambda e: e.memset(ident[:], 0.0), writes=["ident"])
        S.op("gpsimd", lambda e: e.affine_select(out=ident[:], in_=ident[:], pattern=[[-1, 128]], compare_op=ALU.not_equal,
                                                 fill=1.0, base=0, channel_multiplier=1), reads=["ident"], writes=["ident"])
        S.op("vector", lambda e: e.memset(ones[:], 1.0), writes=["ones"])
        S.op("vector", lambda e: e.memset(ones_bf[:], 1.0), writes=["ones_bf"])
        S.op("vector", lambda e: e.memset(halfpi[:], math.pi / 2), writes=["halfpi"])
        S.op("vector", lambda e: e.memset(carry_re[:], 0.0), writes=["carry"])
        S.op("vector", lambda e: e.memset(carry_im[:], 0.0), writes=["carry"])
        S.op("scalar", lambda e: e.activation(out=sinkE[0:64], in_=sinkE[0:64], func=AF.Exp), reads=["sinkE"], writes=["sinkE"])
        S.op("gpsimd", lambda e: e.memset(E_m1[:], 0.0), writes=["E"])
        S.op("gpsimd", lambda e: e.memset(E_m0[:], 0.0), writes=["E"])
        S.op("gpsimd", lambda e: e.memset(E_mc[:], 0.0), writes=["E"])
        S.op("gpsimd", lambda e: e.memset(metaK[:], 0.0), writes=["metaK"])
        S.op("gpsimd", lambda e: e.memset(metaV[:], 0.0), writes=["metaV"])

        oh = uview(0, [33, 700], F32)
        S.dma("sync", lambda e: e.dma_start(out=oh, in_=oh_d), writes=["oh"])

        def build_E(Etile, base, off0, K, Q):
            PT = psall[:, 0:4, :].rearrange("p b (q h) -> p (b q) h", h=16)
            for q in range(Q):
                S.op("tensor", lambda e, q=q: e.matmul(PT[0:K, q, :], lhsT=oh[:, base + off0 - q: base + off0 - q + K], rhs=relb[:, :],
                                                       start=True, stop=True),
                     reads=["oh", "relb"], writes=PK[0:4])
            S.op("scalar", lambda e: e.activation(out=Etile[0:K, :, 0:Q], in_=PT[0:K, 0:Q, :].rearrange("p q h -> p h q"), func=AF.Exp),
                 reads=PK[0:4], writes=["E"])

        build_E(E_cur, OH_CUR, 127, 128, 128)
        build_E(E_prev, OH_PREV, 127, 128, 128)
        build_E(E_m1, OH_M1, 127, 16, 128)
        build_E(E_m0, OH_M0, 15, 16, 16)
        S.op("tensor", lambda e: e.matmul(PS[4][0:16, 0:16], lhsT=oh[:, OH_MC:OH_MC + 16], rhs=relb[:, :], start=True, stop=True),
             reads=["oh", "relb"], writes=[PK[4]])
        S.op("scalar", lambda e: e.activation(out=E_mc[0:16, :], in_=PS[4][0:16, 0:16], func=AF.Exp), reads=[PK[4]], writes=["E"])

        if debug:
            S.dma("sync", lambda e: e.dma_start(out=dbg[:, 7, 0, 0:128], in_=E_cur[:, 0, :]), reads=["E"], writes=["dbgE"])
            S.dma("sync", lambda e: e.dma_start(out=dbg[:, 7, 1, 0:128], in_=E_prev[:, 0, :]), reads=["E"], writes=["dbgE"])
            S.dma("sync", lambda e: e.dma_start(out=dbg[:16, 7, 2, 0:128], in_=E_m1[0:16, 0, :]), reads=["E"], writes=["dbgE"])
            S.dma("sync", lambda e: e.dma_start(out=dbg[:, 7, 3, 0:128], in_=E_cur[:, 5, :]), reads=["E"], writes=["dbgE"])
        lam = uview(4 * KB, [128, 2, 3, 32], F32)
        bm = uview(8 * KB, [128, 2, 2, 32, 16], F32)
        cm = uview(16 * KB, [128, 2, 2, 8, 64], F32)
        S.dma("sync", lambda e: e.dma_start(out=lam, in_=lam_d), writes=["lam"])
        S.dma("sync", lambda e: e.dma_start(out=bm, in_=bmat_d), writes=["bm"])
        S.dma("sync", lambda e: e.dma_start(out=cm, in_=cmat_d), writes=["cm"])
        tmp = uview(24 * KB, [128, 16, 32], F32)
        tmpi = uview(26 * KB, [128, 32], I32)
        bbar = uview(28 * KB, [128, 2, 32, 16], F32)
        bexp = uview(32 * KB, [128, 2, 32, 32], F32)
        cexp = uview(40 * KB, [128, 2, 8, 128], F32)
        btmp = uview(48 * KB, [128, 2, 32, 16], F32)

        def V(fn, reads, writes):
            S.op("vector", fn, reads=reads, writes=writes)

        for l in range(n_layers):
            lr, li, ls = lam[:, l, 0, :], lam[:, l, 1, :], lam[:, l, 2, :]
            T = lambda i: tmp[:, i, :]
            K_ = ["ssmtmp"]
            S.op("scalar", lambda e, ls=ls: e.activation(out=T(0), in_=ls, func=AF.Exp), reads=["lam"], writes=K_)
            V(lambda e, lr=lr: e.tensor_tensor(out=T(1), in0=lr, in1=T(0), op=ALU.mult), ["lam"] + K_, K_)
            S.op("scalar", lambda e: e.activation(out=T(1), in_=T(1), func=AF.Exp), reads=K_, writes=K_)
            V(lambda e, li=li: e.tensor_tensor(out=T(2), in0=li, in1=T(0), op=ALU.mult), ["lam"] + K_, K_)
            V(lambda e: e.tensor_scalar(out=T(3), in0=T(2), scalar1=1.0 / (2 * math.pi), scalar2=None, op0=ALU.mult), K_, K_)
            V(lambda e: e.tensor_copy(out=tmpi, in_=T(3)), K_, K_)
            V(lambda e: e.tensor_copy(out=T(3), in_=tmpi), K_, K_)
            V(lambda e: e.scalar_tensor_tensor(out=T(2), in0=T(3), scalar=-2 * math.pi, in1=T(2), op0=ALU.mult, op1=ALU.add), K_, K_)
            S.op("scalar", lambda e: e.activation(out=T(4), in_=T(2), func=AF.Sin, scale=0.5), reads=K_, writes=K_)
            S.op("scalar", lambda e: e.activation(out=T(3), in_=T(2), func=AF.Sin, scale=0.25), reads=K_, writes=K_)
            V(lambda e: e.tensor_tensor(out=T(5), in0=T(3), in1=T(3), op=ALU.mult), K_, K_)
            V(lambda e: e.tensor_scalar(out=T(5), in0=T(5), scalar1=-2.0, scalar2=1.0, op0=ALU.mult, op1=ALU.add), K_, K_)
            V(lambda e: e.tensor_tensor(out=T(6), in0=T(4), in1=T(5), op=ALU.mult), K_, K_)
            V(lambda e: e.tensor_scalar(out=T(6), in0=T(6), scalar1=2.0, scalar2=None, op0=ALU.mult), K_, K_)
            V(lambda e: e.tensor_tensor(out=T(7), in0=T(4), in1=T(4), op=ALU.mult), K_, K_)
            V(lambda e: e.tensor_scalar(out=T(7), in0=T(7), scalar1=-2.0, scalar2=1.0, op0=ALU.mult, op1=ALU.add), K_, K_)
            a1r, a1i = apr[:, l, :, 0], api[:, l, :, 0]
            V(lambda e, a1r=a1r: e.tensor_tensor(out=a1r, in0=T(1), in1=T(7), op=ALU.mult), K_, ["apow"])
            V(lambda e, a1i=a1i: e.tensor_tensor(out=a1i, in0=T(1), in1=T(6), op=ALU.mult), K_, ["apow"])

            def cmul(orr, oi, xr, xi, yr, yi):
                V(lambda e: e.tensor_tensor(out=T(8), in0=xr, in1=yr, op=ALU.mult), ["apow"] + K_, K_)
                V(lambda e: e.tensor_tensor(out=T(9), in0=xi, in1=yi, op=ALU.mult), ["apow"] + K_, K_)
                V(lambda e: e.tensor_tensor(out=T(10), in0=xr, in1=yi, op=ALU.mult), ["apow"] + K_, K_)
                V(lambda e: e.tensor_tensor(out=T(11), in0=xi, in1=yr, op=ALU.mult), ["apow"] + K_, K_)
                V(lambda e: e.tensor_tensor(out=orr, in0=T(8), in1=T(9), op=ALU.subtract), K_, ["apow"])
                V(lambda e: e.tensor_tensor(out=oi, in0=T(10), in1=T(11), op=ALU.add), K_, ["apow"])

            for i in range(1, NE):
                e_ = EXPS[i]
                if e_ <= 8:
                    j = EIDX[e_ - 1]
                    cmul(apr[:, l, :, i], api[:, l, :, i], apr[:, l, :, j], api[:, l, :, j], a1r, a1i)
                else:
                    j = EIDX[e_ // 2]
                    cmul(apr[:, l, :, i], api[:, l, :, i], apr[:, l, :, j], api[:, l, :, j], apr[:, l, :, j], api[:, l, :, j])
            V(lambda e, a1r=a1r: e.tensor_scalar(out=T(8), in0=a1r, scalar1=-1.0, scalar2=None, op0=ALU.add), ["apow"] + K_, K_)
            V(lambda e, lr=lr: e.tensor_tensor(out=T(9), in0=lr, in1=lr, op=ALU.mult), ["lam"] + K_, K_)
            V(lambda e, li=li: e.tensor_tensor(out=T(10), in0=li, in1=li, op=ALU.mult), ["lam"] + K_, K_)
            V(lambda e: e.tensor_tensor(out=T(9), in0=T(9), in1=T(10), op=ALU.add), K_, K_)
            V(lambda e: e.reciprocal(out=T(9), in_=T(9)), K_, K_)
            V(lambda e, lr=lr: e.tensor_tensor(out=T(10), in0=T(8), in1=lr, op=ALU.mult), ["lam"] + K_, K_)
            V(lambda e, li=li, a1i=a1i: e.tensor_tensor(out=T(11), in0=a1i, in1=li, op=ALU.mult), ["lam", "apow"] + K_, K_)
            V(lambda e: e.tensor_tensor(out=T(10), in0=T(10), in1=T(11), op=ALU.add), K_, K_)
            V(lambda e: e.tensor_tensor(out=T(12), in0=T(10), in1=T(9), op=ALU.mult), K_, K_)
            V(lambda e, lr=lr, a1i=a1i: e.tensor_tensor(out=T(10), in0=a1i, in1=lr, op=ALU.mult), ["lam", "apow"] + K_, K_)
            V(lambda e, li=li: e.tensor_tensor(out=T(11), in0=T(8), in1=li, op=ALU.mult), ["lam"] + K_, K_)
            V(lambda e: e.tensor_tensor(out=T(10), in0=T(10), in1=T(11), op=ALU.subtract), K_, K_)
            V(lambda e: e.tensor_tensor(out=T(13), in0=T(10), in1=T(9), op=ALU.mult), K_, K_)
            zr_b = T(12).unsqueeze(2).to_broadcast([128, 32, 16])
            zi_b = T(13).unsqueeze(2).to_broadcast([128, 32, 16])
            bre, bim = bm[:, l, 0], bm[:, l, 1]
            V(lambda e: e.tensor_tensor(out=bbar[:, 0], in0=bre, in1=zr_b, op=ALU.mult), ["bm"] + K_, ["bbar"])
            V(lambda e: e.tensor_tensor(out=btmp[:, 0], in0=bim, in1=zi_b, op=ALU.mult), ["bm"] + K_, ["btmp"])
            V(lambda e: e.tensor_tensor(out=bbar[:, 0], in0=bbar[:, 0], in1=btmp[:, 0], op=ALU.subtract), ["bbar", "btmp"], ["bbar"])
            V(lambda e: e.tensor_tensor(out=bbar[:, 1], in0=bim, in1=zr_b, op=ALU.mult), ["bm"] + K_, ["bbar"])
            V(lambda e: e.tensor_tensor(out=btmp[:, 1], in0=bre, in1=zi_b, op=ALU.mult), ["bm"] + K_, ["btmp"])
            V(lambda e: e.tensor_tensor(out=bbar[:, 1], in0=bbar[:, 1], in1=btmp[:, 1], op=ALU.add), ["bbar", "btmp"], ["bbar"])
            V(lambda e: e.memset(bexp, 0.0), [], ["bexp"])
            for ri in range(2):
                V(lambda e, ri=ri: e.tensor_copy(out=bexp[0:64, ri, :, 0:16], in_=bbar[0:64, ri]), ["bbar"], ["bexp"])
                V(lambda e, ri=ri: e.tensor_copy(out=bexp[64:128, ri, :, 16:32], in_=bbar[64:128, ri]), ["bbar"], ["bexp"])
            for ri in range(2):
                sgn = 1.0 if ri == 0 else -1.0
                for g2 in range(2):
                    V(lambda e, ri=ri, g2=g2, sgn=sgn: e.tensor_scalar(out=cexp[:, ri, :, g2 * 64:(g2 + 1) * 64], in0=cm[:, l, ri], scalar1=g2m[:, g2:g2 + 1],
                                                                         scalar2=sgn, op0=ALU.mult, op1=ALU.mult), ["cm", "g2m"], ["cexp"])
            for c in range(8):
                for ri in range(2):
                    pt, pk = next_ps(4, 8)
                    S.op("tensor", lambda e, c=c, ri=ri, pt=pt: e.transpose(out=pt[:, 0:128], in_=bexp[:, ri, 4 * c:4 * c + 4, :].rearrange("p a b -> p (a b)"),
                                                                            identity=ident[:]), reads=["bexp", "ident"], writes=[pk])
                    S.op("scalar", lambda e, c=c, ri=ri, pt=pt: e.activation(out=TB[:, l, c, ri, :], in_=pt[:, 0:128], func=AF.Copy), reads=[pk], writes=["TB"])
                    pt, pk = next_ps(4, 8)
                    S.op("tensor", lambda e, c=c, ri=ri, pt=pt: e.transpose(out=pt[:, 0:128], in_=cexp[:, ri, c, :], identity=ident[:]),
                         reads=["cexp", "ident"], writes=[pk])
                    S.op("scalar", lambda e, c=c, ri=ri, pt=pt: e.activation(out=CE[:, l, c, ri, :], in_=pt[:, 0:128], func=AF.Copy), reads=[pk], writes=["CE"])

        SETUP_KEYS = ["oh", "lam", "bm", "cm", "ssmtmp", "bbar", "bexp", "cexp", "btmp"]

        def layer_norm(N, gi, bi):
            sq = [uview(0, [128, NT], F32), uview(2 * KB, [128, NT], F32)]
            mean = uview(4 * KB, [128, NT], F32)
            rstd = uview(6 * KB, [128, NT], F32)
            t1 = [uview(8 * KB, [128, NT], F32), uview(10 * KB, [128, NT], F32)]
            for c in range(KC):
                s_ = sq[c % 2]
                S.op("scalar", lambda e, c=c, s_=s_: e.activation(out=s_[:, :N], in_=h[:, c, :N], func=AF.Square),
                     reads=["h"], writes=["lnsq%d" % (c % 2)])
                S.op("tensor", lambda e, c=c: e.matmul(PS[4][:, :N], lhsT=ones[:], rhs=h[:, c, :N], start=(c == 0), stop=(c == KC - 1)),
                     reads=["h", "ones"], writes=[PK[4]])
                S.op("tensor", lambda e, c=c, s_=s_: e.matmul(PS[5][:, :N], lhsT=ones[:], rhs=s_[:, :N], start=(c == 0), stop=(c == KC - 1)),
                     reads=["lnsq%d" % (c % 2), "ones"], writes=[PK[5]])
            S.op("scalar", lambda e: e.activation(out=mean[:, :N], in_=PS[4][:, :N], func=AF.Identity, scale=1.0 / D), reads=[PK[4]], writes=["lnmean"])
            V(lambda e: e.tensor_tensor(out=rstd[:, :N], in0=mean[:, :N], in1=mean[:, :N], op=ALU.mult), ["lnmean"], ["lnrstd"])
            V(lambda e: e.scalar_tensor_tensor(out=rstd[:, :N], in0=PS[5][:, :N], scalar=1.0 / D, in1=rstd[:, :N], op0=ALU.mult, op1=ALU.subtract),
              [PK[5], "lnrstd"], ["lnrstd"])
            V(lambda e: e.tensor_scalar(out=rstd[:, :N], in0=rstd[:, :N], scalar1=EPS, scalar2=None, op0=ALU.add), ["lnrstd"], ["lnrstd"])
            S.op("scalar", lambda e: e.activation(out=rstd[:, :N], in_=rstd[:, :N], func=AF.Sqrt), reads=["lnrstd"], writes=["lnrstd"])
            V(lambda e: e.reciprocal(out=rstd[:, :N], in_=rstd[:, :N]), ["lnrstd"], ["lnrstd"])
            for c in range(KC):
                t_ = t1[c % 2]
                tk = "lnt%d" % (c % 2)
                V(lambda e, c=c, t_=t_: e.tensor_tensor(out=t_[:, :N], in0=h[:, c, :N], in1=mean[:, :N], op=ALU.subtract), ["h", "lnmean"], [tk])
                V(lambda e, c=c, t_=t_: e.tensor_tensor(out=t_[:, :N], in0=t_[:, :N], in1=rstd[:, :N], op=ALU.mult), [tk, "lnrstd"], [tk])
                V(lambda e, c=c, t_=t_: e.tensor_scalar(out=h[:, c, :N], in0=t_[:, :N], scalar1=lnp[:, gi, c:c + 1], scalar2=lnp[:, bi, c:c + 1],
                                                        op0=ALU.mult, op1=ALU.add), [tk, "lnp"], ["h"])
                S.op("scalar", lambda e, c=c: e.activation(out=hb[:, c, :N], in_=h[:, c, :N], func=AF.Copy), reads=["h"], writes=["hb"])

        def mm16(out_ps, pk, lhs_fn, rhs_fn, nk, extra_reads):
            for k in range(nk):
                S.op("tensor", lambda e, k=k: e.matmul(out_ps, lhsT=lhs_fn(k), rhs=rhs_fn(k), start=(k == 0), stop=(k == nk - 1)),
                     reads=extra_reads, writes=[pk])

        def dbg_dump(slot, src_fn, nchunks, N, reads, parts=128):
            if not debug:
                return
            for c in range(nchunks):
                t_ = uview(62 * KB, [128, NT], F32)
                V(lambda e, c=c: e.tensor_copy(out=t_[:parts, :N], in_=src_fn(c)), reads + ["dbgt"], ["dbgt"])
                S.dma("sync", lambda e, c=c: e.dma_start(out=dbg[:parts, slot, c, dbgcol[0]:dbgcol[0] + N], in_=t_[:parts, :N]), reads=["dbgt"], writes=["dbgt2"])

        dbgcol = [0]

        cur_t = [0]

        def chk(k):
            if stop_after == k and cur_t[0] == n_tiles - 1:
                raise _Stop()

        def tile_layer(l, t, N, c0):
            nblk = max(1, N // 128)
            nch = N // 8
            L = nch + 1
            u_bf = uview(0, [128, 8, NT], BF16)
            yg = uview(8 * KB, [128, 8, NT], BF16)
            Xre = uview(16 * KB, [128, 4, NT], F32)
            Xim = uview(24 * KB, [128, 4, NT], F32)
            Hre = uview(32 * KB, [128, 4, NT], BF16)
            Him = uview(36 * KB, [128, 4, NT], BF16)
            tt = [uview(40 * KB + i * KB, [128, 4, 64], F32) for i in range(4)]
            SA = [[uview(44 * KB + (2 * a + b) * 1040, [128, 4, 65], F32) for b in range(2)] for a in range(2)]
            ytmp = [uview(50 * KB, [128, NT], F32), uview(52 * KB, [128, NT], F32)]
            yssm = uview(54 * KB, [128, 8, NT], BF16)
            barrier(["a", "dbgt"] + LNK + SETUP_KEYS, SSMK + ["yssm"])
            for j in range(2):
                sl, sk = load_slab([(lambda s: v3(s, 0, 16, 512), w_in[l, :, 1536 + j * 512: 1536 + (j + 1) * 512].rearrange("(k p) n -> p k n", p=128))])
                for cc in range(4):
                    c = 4 * j + cc
                    pt, pk = next_ps(0, 4)
                    mm16(pt[:, :N], pk, lambda k, sl=sl, cc=cc: v3(sl, 0, 16, 512)[:, k, cc * 128:(cc + 1) * 128], lambda k: hb[:, k, :N], KC, [sk, "hb"])
                    S.op("scalar", lambda e, c=c, pt=pt: e.activation(out=u_bf[:, c, :N], in_=pt[:, :N], func=AF.Copy), reads=[pk], writes=["u_bf"])
            dbg_dump(0, lambda c: u_bf[:, c, :N], 8, N, ["u_bf"])
            chk(2)
            for c in range(8):
                for q in range(4):
                    for ri, Xt, xk in ((0, Xre, "Xre"), (1, Xim, "Xim")):
                        pt, pk = next_ps(0, 4)
                        S.op("tensor", lambda e, q=q, ri=ri, pt=pt, c=c: e.matmul(pt[:, :N], lhsT=TB[32 * q:32 * q + 32, l, c, ri, :], rhs=u_bf[32 * q:32 * q + 32, c, :N],
                                                                                    start=True, stop=True, tile_position=(32 * q, 0)),
                             reads=["TB", "u_bf"], writes=[pk])
                        S.op("scalar", lambda e, q=q, pt=pt, Xt=Xt: e.activation(out=Xt[:, q, :N], in_=pt[:, :N], func=AF.Copy), reads=[pk], writes=[xk])
                Xr4 = Xre[:, :, :N].rearrange("p g (c j) -> p g c j", j=8)
                Xi4 = Xim[:, :, :N].rearrange("p g (c j) -> p g c j", j=8)
                Hr4 = Hre[:, :, :N].rearrange("p g (c j) -> p g c j", j=8)
                Hi4 = Him[:, :, :N].rearrange("p g (c j) -> p g c j", j=8)

                def ab(e_, which, n_):
                    src = apr if which == 0 else api
                    return src[:, l, 4 * c:4 * c + 4, EIDX[e_]].unsqueeze(2).to_broadcast([128, 4, n_])

                T0_, T1_, T2_, T3_ = [x[:, :, :nch] for x in tt]
                TK = ["tt0", "tt1", "tt2", "tt3"]
                for j in range(1, 8):
                    V(lambda e, j=j: e.tensor_tensor(out=T0_, in0=Xr4[:, :, :, j - 1], in1=ab(1, 0, nch), op=ALU.mult), ["Xre", "apow"], [TK[0]])
                    V(lambda e, j=j: e.tensor_tensor(out=T1_, in0=Xi4[:, :, :, j - 1], in1=ab(1, 1, nch), op=ALU.mult), ["Xim", "apow"], [TK[1]])
                    V(lambda e, j=j: e.tensor_tensor(out=T2_, in0=Xi4[:, :, :, j - 1], in1=ab(1, 0, nch), op=ALU.mult), ["Xim", "apow"], [TK[2]])
                    V(lambda e, j=j: e.tensor_tensor(out=T3_, in0=Xr4[:, :, :, j - 1], in1=ab(1, 1, nch), op=ALU.mult), ["Xre", "apow"], [TK[3]])
                    V(lambda e, j=j: e.tensor_tensor(out=Xr4[:, :, :, j], in0=Xr4[:, :, :, j], in1=T0_, op=ALU.add), ["Xre", TK[0]], ["Xre"])
                    V(lambda e, j=j: e.tensor_tensor(out=Xr4[:, :, :, j], in0=Xr4[:, :, :, j], in1=T1_, op=ALU.subtract), ["Xre", TK[1]], ["Xre"])
                    V(lambda e, j=j: e.tensor_tensor(out=Xi4[:, :, :, j], in0=Xi4[:, :, :, j], in1=T2_, op=ALU.add), ["Xim", TK[2]], ["Xim"])
                    V(lambda e, j=j: e.tensor_tensor(out=Xi4[:, :, :, j], in0=Xi4[:, :, :, j], in1=T3_, op=ALU.add), ["Xim", TK[3]], ["Xim"])
                A, B = SA[0], SA[1]
                AK, BK = ["SA0r", "SA0i"], ["SA1r", "SA1i"]
                V(lambda e: e.tensor_copy(out=A[0][:, :, 0], in_=carry_re[:, l, 4 * c:4 * c + 4]), ["carry"], [AK[0]])
                V(lambda e: e.tensor_copy(out=A[1][:, :, 0], in_=carry_im[:, l, 4 * c:4 * c + 4]), ["carry"], [AK[1]])
                V(lambda e: e.tensor_copy(out=A[0][:, :, 1:L], in_=Xr4[:, :, :, 7]), ["Xre"], [AK[0]])
                V(lambda e: e.tensor_copy(out=A[1][:, :, 1:L], in_=Xi4[:, :, :, 7]), ["Xim"], [AK[1]])
                d = 1
                while d < L:
                    m = L - d
                    e8 = 8 * d
                    V(lambda e, A=A, B=B, d=d: e.tensor_copy(out=B[0][:, :, 0:d], in_=A[0][:, :, 0:d]), [AK[0]], [BK[0]])
                    V(lambda e, A=A, B=B, d=d: e.tensor_copy(out=B[1][:, :, 0:d], in_=A[1][:, :, 0:d]), [AK[1]], [BK[1]])
                    V(lambda e, A=A, m=m, e8=e8: e.tensor_tensor(out=tt[0][:, :, :m], in0=A[0][:, :, 0:m], in1=ab(e8, 0, m), op=ALU.mult), [AK[0], "apow"], [TK[0]])
                    V(lambda e, A=A, m=m, e8=e8: e.tensor_tensor(out=tt[1][:, :, :m], in0=A[1][:, :, 0:m], in1=ab(e8, 1, m), op=ALU.mult), [AK[1], "apow"], [TK[1]])
                    V(lambda e, A=A, m=m, e8=e8: e.tensor_tensor(out=tt[2][:, :, :m], in0=A[1][:, :, 0:m], in1=ab(e8, 0, m), op=ALU.mult), [AK[1], "apow"], [TK[2]])
                    V(lambda e, A=A, m=m, e8=e8: e.tensor_tensor(out=tt[3][:, :, :m], in0=A[0][:, :, 0:m], in1=ab(e8, 1, m), op=ALU.mult), [AK[0], "apow"], [TK[3]])
                    V(lambda e, A=A, B=B, d=d, m=m: e.tensor_tensor(out=B[0][:, :, d:L], in0=A[0][:, :, d:L], in1=tt[0][:, :, :m], op=ALU.add), [AK[0], TK[0]], [BK[0]])
                    V(lambda e, B=B, d=d, m=m: e.tensor_tensor(out=B[0][:, :, d:L], in0=B[0][:, :, d:L], in1=tt[1][:, :, :m], op=ALU.subtract), [BK[0], TK[1]], [BK[0]])
                    V(lambda e, A=A, B=B, d=d, m=m: e.tensor_tensor(out=B[1][:, :, d:L], in0=A[1][:, :, d:L], in1=tt[2][:, :, :m], op=ALU.add), [AK[1], TK[2]], [BK[1]])
                    V(lambda e, B=B, d=d, m=m: e.tensor_tensor(out=B[1][:, :, d:L], in0=B[1][:, :, d:L], in1=tt[3][:, :, :m], op=ALU.add), [BK[1], TK[3]], [BK[1]])
                    A, B = B, A
                    AK, BK = BK, AK
                    d *= 2
                V(lambda e, A=A: e.tensor_copy(out=carry_re[:, l, 4 * c:4 * c + 4], in_=A[0][:, :, nch]), [AK[0]], ["carry"])
                V(lambda e, A=A: e.tensor_copy(out=carry_im[:, l, 4 * c:4 * c + 4], in_=A[1][:, :, nch]), [AK[1]], ["carry"])
                for j in range(8):
                    V(lambda e, A=A, j=j: e.tensor_tensor(out=T0_, in0=A[0][:, :, 0:nch], in1=ab(j + 1, 0, nch), op=ALU.mult), [AK[0], "apow"], [TK[0]])
                    V(lambda e, A=A, j=j: e.tensor_tensor(out=T1_, in0=A[1][:, :, 0:nch], in1=ab(j + 1, 1, nch), op=ALU.mult), [AK[1], "apow"], [TK[1]])
                    V(lambda e, A=A, j=j: e.tensor_tensor(out=T2_, in0=A[1][:, :, 0:nch], in1=ab(j + 1, 0, nch), op=ALU.mult), [AK[1], "apow"], [TK[2]])
                    V(lambda e, A=A, j=j: e.tensor_tensor(out=T3_, in0=A[0][:, :, 0:nch], in1=ab(j + 1, 1, nch), op=ALU.mult), [AK[0], "apow"], [TK[3]])
                    V(lambda e: e.tensor_tensor(out=T0_, in0=T0_, in1=T1_, op=ALU.subtract), [TK[0], TK[1]], [TK[0]])
                    V(lambda e, j=j: e.tensor_tensor(out=Hr4[:, :, :, j], in0=Xr4[:, :, :, j], in1=T0_, op=ALU.add), ["Xre", TK[0]], ["Hre"])
                    V(lambda e: e.tensor_tensor(out=T2_, in0=T2_, in1=T3_, op=ALU.add), [TK[2], TK[3]], [TK[2]])
                    V(lambda e, j=j: e.tensor_tensor(out=Hi4[:, :, :, j], in0=Xi4[:, :, :, j], in1=T2_, op=ALU.add), ["Xim", TK[2]], ["Him"])
                pt, pk = next_ps(4, 8)
                for q in range(4):
                    S.op("tensor", lambda e, q=q, pt=pt, c=c: e.matmul(pt[32 * q:32 * q + 32, :N], lhsT=CE[:, l, c, 0, 32 * q:32 * q + 32], rhs=Hre[:, q, :N],
                                                                        start=True, stop=False, tile_position=(0, 32 * q)), reads=["CE", "Hre"], writes=[pk])
                    S.op("tensor", lambda e, q=q, pt=pt, c=c: e.matmul(pt[32 * q:32 * q + 32, :N], lhsT=CE[:, l, c, 1, 32 * q:32 * q + 32], rhs=Him[:, q, :N],
                                                                        start=False, stop=True, tile_position=(0, 32 * q)), reads=["CE", "Him"], writes=[pk])
                yt = ytmp[c % 2]
                yk = "ytmp%d" % (c % 2)
                V(lambda e, c=c, pt=pt, yt=yt: e.scalar_tensor_tensor(out=yt[:, :N], in0=u_bf[:, c, :N], scalar=dvec[:, l, c:c + 1], in1=pt[:, :N], op0=ALU.mult, op1=ALU.add),
                  ["u_bf", "dvec", pk], [yk])
                S.op("scalar", lambda e, c=c, yt=yt: e.activation(out=yg[:, c, :N], in_=yt[:, :N], func=AF.Gelu_apprx_tanh), reads=[yk], writes=["yg"])
            chk(3)
            sl, sk = load_slab([(lambda s: v3(s, 0, 8, 1024), w_glu[l].rearrange("(k p) n -> p k n", p=128))])
            for oc in range(8):
                pt, pk = next_ps(0, 4)
                mm16(pt[:, :N], pk, lambda k, sl=sl, oc=oc: v3(sl, 0, 8, 1024)[:, k, oc * 128:(oc + 1) * 128], lambda k: yg[:, k, :N], 8, [sk, "yg"])
                yt = ytmp[oc % 2]
                yk = "ytmp%d" % (oc % 2)
                S.op("scalar", lambda e, pt=pt, yt=yt: e.activation(out=yt[:, :N], in_=pt[:, :N], func=AF.Sigmoid), reads=[pk], writes=[yk])
                V(lambda e, oc=oc, yt=yt: e.tensor_tensor(out=yssm[:, oc, :N], in0=yg[:, oc, :N], in1=yt[:, :N], op=ALU.mult), ["yg", yk], ["yssm"])
            dbg_dump(1, lambda c: yssm[:, c, :N], 8, N, ["yssm"])
            chk(4)

            qT = uview(0, [128, 16, NT], BF16)
            kT2 = uview(16 * KB, [128, 4, NT], BF16)
            vt = uview(20 * KB, [128, 4, 4, 128], BF16)
            yat = uview(24 * KB, [64, 16, NT], BF16)
            ex = [uview(40 * KB + (i % 4) * 2 * KB, [128, NT], F32) for i in range(6)]
            pT = [uview(48 * KB + (i % 4) * KB, [128, NT], BF16) for i in range(6)]
            rd = [sb_rl0[0:64, :], sb_rl1[0:64, :]]
            barrier(SSMK + ["dbgt", "rl0", "rl1"], ATK + ATT)
            S.op("gpsimd", lambda e: e.memset(qT, 0.0), writes=["qT"])
            for j in range(2):
                sl, sk = load_slab([(lambda s: v3(s, 0, 16, 512), w_in[l, :, j * 512:(j + 1) * 512].rearrange("(k p) n -> p k n", p=128))])
                for cc in range(4):
                    c = 4 * j + cc
                    pt, pk = next_ps(0, 4)
                    mm16(pt[:, :N], pk, lambda k, sl=sl, cc=cc: v3(sl, 0, 16, 512)[:, k, cc * 128:(cc + 1) * 128], lambda k: hb[:, k, :N], KC, [sk, "hb"])
                    S.op("scalar", lambda e, c=c, pt=pt: e.activation(out=qT[0:64, 2 * c, :N], in_=pt[0:64, :N], func=AF.Copy), reads=[pk], writes=["qT"])
                    S.op("scalar", lambda e, c=c, pt=pt: e.activation(out=qT[64:128, 2 * c + 1, :N], in_=pt[64:128, :N], func=AF.Copy), reads=[pk], writes=["qT"])
            S.op("gpsimd", lambda e: e.memset(vt, 0.0), writes=["vt"])
            sl, sk = load_slab([(lambda s: v3(s, 0, 16, 512), w_in[l, :, 1024:1536].rearrange("(k p) n -> p k n", p=128))])
            for kvh in range(4):
                pt, pk = next_ps(0, 4)
                for half in range(2):
                    for k in range(KC):
                        S.op("tensor", lambda e, k=k, half=half, kvh=kvh, pt=pt, sl=sl: e.matmul(pt[64 * half:64 * half + 64, :N], lhsT=v3(sl, 0, 16, 512)[:, k, kvh * 64:(kvh + 1) * 64],
                                                                                              rhs=hb[:, k, :N], start=(k == 0), stop=(k == KC - 1)),
                             reads=[sk, "hb"], writes=[pk])
                S.op("scalar", lambda e, kvh=kvh, pt=pt: e.activation(out=kT2[:, kvh, :N], in_=pt[:, :N], func=AF.Copy), reads=[pk], writes=["kT2"])
            nb_tok = min(N, 128)
            for b in range(nblk):
                pt, pk = next_ps(0, 4)
                mm16(pt[:nb_tok, 0:256], pk, lambda k, b=b: hb[:, k, b * 128:b * 128 + nb_tok], lambda k, sl=sl: v3(sl, 0, 16, 512)[:, k, 256:512], KC, [sk, "hb"])
                S.op("scalar", lambda e, b=b, pt=pt: e.activation(out=vt[:nb_tok, b, :, 0:64], in_=pt[:nb_tok, 0:256].rearrange("p (k d) -> p k d", d=64), func=AF.Copy),
                     reads=[pk], writes=["vt"])
            if t == 0:
                S.op("vector", lambda e: e.tensor_copy(out=metaK[:, l, :, 0:16], in_=kT2[:, :, 0:16]), reads=["kT2"], writes=["metaK"])
                S.op("vector", lambda e: e.tensor_copy(out=metaV[0:16, l, :, :], in_=vt[0:16, 0, :, :]), reads=["vt"], writes=["metaV"])
            dbg_dump(2, lambda c: qT[:, c, :N], 8, N, ["qT"])
            chk(5)
            ei = [0]
            if VARIANT == 30 and t == 0:
                S.op("gpsimd", lambda e: e.memset(yat[:, :, :N], 0.0), writes=["yat"])
            for b in range(nblk if not (VARIANT == 30 and t == 0) else 0):
                cols = slice(b * 128, b * 128 + nb_tok)
                gb = (t - 1) * 4 + b + 1 if t > 0 else 0
                for kvh_i in range(4):
                    kvh = 0 if (VARIANT == 20) else kvh_i
                    segs = []
                    if t == 0:
                        segs.append(("m0", 128, lambda half, kvh=kvh: metaK[:, l, kvh, :],
                                     lambda kvh=kvh: metaV[:, l, kvh, :], ["metaK", "metaV"]))
                    else:
                        segs.append(("m1" if gb == 1 else "mc", 128, lambda half, kvh=kvh: metaK[:, l, kvh, :],
                                     lambda kvh=kvh: metaV[:, l, kvh, :], ["metaK", "metaV"]))
                        if gb > 1:
                            if b == 0:
                                segs.append(("prev", 128, lambda half, kvh=kvh: prevK[:, l, kvh, :],
                                             lambda kvh=kvh: prevV[:, l, kvh, :], ["prevK", "prevV"]))
                            else:
                                segs.append(("prev", 128, lambda half, kvh=kvh, b=b: kT2[:, kvh, (b - 1) * 128:b * 128],
                                             lambda kvh=kvh, b=b: vt[:, b - 1, kvh, :], ["kT2", "vt"]))
                        segs.append(("cur", 128, lambda half, kvh=kvh, b=b: kT2[:, kvh, b * 128:(b + 1) * 128],
                                     lambda kvh=kvh, b=b: vt[:, b, kvh, :], ["kT2", "vt"]))
                    W_ = 4 * nb_tok
                    po, pok = PS[6], PK[6]
                    pd, pdk = PS[7], PK[7]
                    nseg = len(segs)
                    pts = []
                    for si, (kind, nk, kf, vf, rk) in enumerate(segs):
                        sps, spk = next_ps(0, 1) if VARIANT in (10, 12) else next_ps(0, 6)
                        for g in range(4):
                            hh = 4 * kvh + g
                            S.op("tensor", lambda e, g=g, hh=hh, kf=kf, sps=sps, nk=nk, cols=cols: e.matmul(
                                sps[0:nk, g * nb_tok:(g + 1) * nb_tok], lhsT=kf(2), rhs=qT[:, hh, cols], start=True, stop=True),
                                 reads=rk + ["qT"], writes=[spk])
                        chk(60)
                        if kvh_i == 1:
                            chk(70)
                        xi = ei[0] % 4
                        ei[0] += 1
                        S.op("scalar", lambda e, sps=sps, nk=nk, xi=xi: e.activation(out=ex[xi][0:nk, :W_], in_=sps[0:nk, :W_], func=AF.Exp, scale=0.125),
                             reads=[spk], writes=["ex%d" % xi])
                        chk(61)
                        if kvh == 1:
                            chk(71)
                        if kind == "mc":
                            Et = E_mc[:, 4 * kvh:4 * kvh + 4].unsqueeze(2).to_broadcast([128, 4, nb_tok])
                            S.op("gpsimd", lambda e, xi=xi, nk=nk, Et=Et: e.tensor_tensor(out=pT[xi][0:nk, :W_].rearrange("p (g q) -> p g q", g=4),
                                                                               in0=ex[xi][0:nk, :W_].rearrange("p (g q) -> p g q", g=4), in1=Et, op=ALU.mult),
                                 reads=["ex%d" % xi, "E"], writes=["pT%d" % xi])
                        else:
                            Esrc = {"cur": E_cur, "prev": E_prev, "m1": E_m1, "m0": E_m0}[kind]
                            Et = Esrc[0:nk, 4 * kvh:4 * kvh + 4, :].rearrange("p g q -> p (g q)")
                            if VARIANT == 1:
                                Et = ex[xi][0:nk, :W_]
                            outp = pT[xi][0:nk, :W_] if VARIANT != 3 else rd[0][0:nk, :W_]
                            S.op("gpsimd", lambda e, xi=xi, nk=nk, Et=Et, outp=outp: e.tensor_tensor(out=outp, in0=ex[xi][0:nk, :W_], in1=Et, op=ALU.mult),
                                 reads=["ex%d" % xi, "E"], writes=["pT%d" % xi])
                        pts.append((xi, nk, vf, rk))
                        chk(62)
                        if kvh == 1:
                            chk(72)
                    if VARIANT == 41:
                        continue
                    for si, (xi, nk, vf, rk) in enumerate(pts):
                        S.op("tensor", lambda e, xi=xi, nk=nk, vf=vf, si=si: e.matmul(po[:, :W_], lhsT=vf(), rhs=pT[xi][0:nk, :W_], start=(si == 0), stop=(si == nseg - 1)),
                             reads=rk + ["pT%d" % xi], writes=[pok])
                    for si, (xi, nk, vf, rk) in enumerate(pts):
                        S.op("tensor", lambda e, xi=xi, nk=nk, si=si: e.matmul(pd[:, :W_], lhsT=ones_bf[0:nk, :], rhs=pT[xi][0:nk, :W_], start=(si == 0), stop=(si == nseg - 1)),
                             reads=["ones_bf", "pT%d" % xi], writes=[pdk])
                    if VARIANT in (42, 43):
                        S.op("tensor", lambda e: e.matmul(pd[:, 256:384], lhsT=hb[:, 0, 0:128] if VARIANT == 43 else metaK[:, l, 0, :], rhs=metaK[:, l, 1, :], start=True, stop=True),
                             reads=["metaK", "hb"], writes=[pdk])
                    chk(63)
                    if kvh == 1:
                        chk(73)
                    if VARIANT in (40, 42, 43):
                        continue
                    r_, o_ = rd[0], rd[1]
                    V(lambda e, r_=r_, kvh=kvh: e.tensor_tensor(out=r_[:, :W_].rearrange("p (g q) -> p g q", g=4), in0=pd[0:64, :W_].rearrange("p (g q) -> p g q", g=4),
                                                                in1=sinkE[0:64, l, 4 * kvh:4 * kvh + 4].unsqueeze(2).to_broadcast([64, 4, nb_tok]), op=ALU.add),
                      [pdk, "sinkE"], ["rl0"])
                    chk(64)
                    if kvh == 1:
                        chk(74)
                    V(lambda e, o_=o_: e.tensor_copy(out=o_[:, :W_], in_=po[0:64, :W_]), [pok], ["rl1"])
                    chk(65)
                    if kvh == 1:
                        chk(75)
                    S.op("scalar", lambda e, r_=r_: e.activation(out=r_[:, :W_], in_=r_[:, :W_], func=AF.Ln), reads=["rl0"], writes=["rl0"])
                    S.op("scalar", lambda e, r_=r_: e.activation(out=r_[:, :W_], in_=r_[:, :W_], func=AF.Exp, scale=-1.0), reads=["rl0"], writes=["rl0"])
                    chk(66)
                    if kvh == 1:
                        chk(76)
                    S.op("gpsimd", lambda e, r_=r_, o_=o_, kvh=kvh, cols=cols: e.tensor_tensor(out=yat[:, 4 * kvh:4 * kvh + 4, cols], in0=o_[:, :W_].rearrange("p (g q) -> p g q", g=4),
                                                                                      in1=r_[:, :W_].rearrange("p (g q) -> p g q", g=4), op=ALU.mult),
                         reads=["rl0", "rl1"], writes=["yat"])
                    chk(67 + kvh)
            if t > 0:
                S.op("vector", lambda e: e.tensor_copy(out=prevK[:, l, :, :], in_=kT2[:, :, 384:512]), reads=["kT2"], writes=["prevK"])
                S.op("vector", lambda e: e.tensor_copy(out=prevV[:, l, :, :], in_=vt[:, 3, :, :]), reads=["vt"], writes=["prevV"])
            if VARIANT == 7:
                dbg_dump(3, lambda c: yat[:, c, :N], 16, N, ["yat"], parts=64)
            chk(6)

            mixed = uview(0, [128, 16, NT], BF16)
            gt = [uview(40 * KB + (i % 2) * 2 * KB, [128, NT], F32) for i in range(4)]
            barrier(ATT + ["dbgt", "qT"], GK)
            for o in range(16):
                sl, sk = load_slab([
                    (lambda s: v3(s, 0, 16, 128), w_in[l, :, 2560 + o * 128: 2560 + (o + 1) * 128].rearrange("(k p) n -> p k n", p=128)),
                    (lambda s: v3(s, 2048, 16, 128), w_in[l, :, 4608 + o * 128: 4608 + (o + 1) * 128].rearrange("(k p) n -> p k n", p=128)),
                    (lambda s: v3(s, 4096, 16, 128, parts=64), w_au[l, :, o * 128:(o + 1) * 128].rearrange("(h d) n -> d h n", d=64)),
                    (lambda s: v3(s, 6144, 8, 128), w_su[l, :, o * 128:(o + 1) * 128].rearrange("(k p) n -> p k n", p=128)),
                ])
                base = 0 if o % 2 == 0 else 4
                pa, pb_, pc, pd_ = PS[base], PS[base + 1], PS[base + 2], PS[base + 3]
                ka, kb, kc_, kd = PK[base], PK[base + 1], PK[base + 2], PK[base + 3]
                mm16(pa[:, :N], ka, lambda k, sl=sl: v3(sl, 0, 16, 128)[:, k, :], lambda k: hb[:, k, :N], 16, [sk, "hb"])
                mm16(pb_[:, :N], kb, lambda k, sl=sl: v3(sl, 2048, 16, 128)[:, k, :], lambda k: hb[:, k, :N], 16, [sk, "hb"])
                mm16(pc[:, :N], kc_, lambda k, sl=sl: v3(sl, 4096, 16, 128, parts=64)[:, k, :], lambda k: yat[:, k, :N], 16, [sk, "yat"])
                mm16(pd_[:, :N], kd, lambda k, sl=sl: v3(sl, 6144, 8, 128)[:, k, :], lambda k: yssm[:, k, :N], 8, [sk, "yssm"])
                i0 = (o % 2) * 2
                g0, g1 = gt[i0], gt[i0 + 1]
                gk0, gk1 = "gt0", "gt1"
                S.op("scalar", lambda e, o=o, pa=pa, g0=g0: e.activation(out=g0[:, :N], in_=pa[:, :N], func=AF.Sigmoid, bias=gateb[:, l, o:o + 1], scale=1.0),
                     reads=[ka, "gateb"], writes=[gk0])
                S.op("scalar", lambda e, o=o, pb_=pb_, g1=g1: e.activation(out=g1[:, :N], in_=pb_[:, :N], func=AF.Sigmoid, bias=gateb[:, l, 16 + o:17 + o], scale=1.0),
                     reads=[kb, "gateb"], writes=[gk1])
                V(lambda e, g0=g0, pc=pc: e.tensor_tensor(out=g0[:, :N], in0=g0[:, :N], in1=pc[:, :N], op=ALU.mult), [gk0, kc_], [gk0])
                V(lambda e, g1=g1, pd_=pd_: e.tensor_tensor(out=g1[:, :N], in0=g1[:, :N], in1=pd_[:, :N], op=ALU.mult), [gk1, kd], [gk1])
                V(lambda e, o=o, g0=g0, g1=g1: e.tensor_tensor(out=mixed[:, o, :N], in0=g0[:, :N], in1=g1[:, :N], op=ALU.add), [gk0, gk1], ["mixed"])
            dbg_dump(4, lambda c: mixed[:, c, :N], 16, N, ["mixed"])
            chk(7)

            for j in range(4):
                sl, sk = load_slab([(lambda s: v3(s, 0, 16, 512), w_o[l, :, j * 512:(j + 1) * 512].rearrange("(k p) n -> p k n", p=128))])
                for cc in range(4):
                    o = 4 * j + cc
                    pt, pk = next_ps(0, 4)
                    mm16(pt[:, :N], pk, lambda k, sl=sl, cc=cc: v3(sl, 0, 16, 512)[:, k, cc * 128:(cc + 1) * 128], lambda k: mixed[:, k, :N], 16, [sk, "mixed"])
                    V(lambda e, o=o, pt=pt: e.scalar_tensor_tensor(out=h[:, o, :N], in0=h[:, o, :N], scalar=ALPHA, in1=pt[:, :N], op0=ALU.mult, op1=ALU.add),
                      ["h", pk], ["h"])
            barrier(ATK + ["dbgt"], LNK)
            layer_norm(N, 2 + 4 * l, 3 + 4 * l)
            dbg_dump(5, lambda c: h[:, c, :N], 16, N, ["h"])
            chk(8)

            a_ = uview(0, [128, 64, NT], BF16)
            rl = [sb_rl0, sb_rl1]
            barrier(LNK + ATK + ATT + GK + SSMK + ["yssm", "dbgt"], ["a"])
            for j in range(16):
                sl, sk = load_slab([(lambda s: v3(s, 0, 16, 512), w_up[l, :, j * 512:(j + 1) * 512].rearrange("(k p) n -> p k n", p=128))])
                for cc in range(4):
                    f = 4 * j + cc
                    pt, pk = next_ps(0, 4)
                    mm16(pt[:, :N], pk, lambda k, sl=sl, cc=cc: v3(sl, 0, 16, 512)[:, k, cc * 128:(cc + 1) * 128], lambda k: hb[:, k, :N], 16, [sk, "hb"])
                    r_ = rl[f % 2]
                    S.op("scalar", lambda e, pt=pt, r_=r_: e.activation(out=r_[:, :N], in_=pt[:, :N], func=AF.Relu), reads=[pk], writes=["rl%d" % (f % 2)])
                    V(lambda e, f=f, r_=r_: e.tensor_tensor(out=a_[:, f, :N], in0=r_[:, :N], in1=r_[:, :N], op=ALU.mult), ["rl%d" % (f % 2)], ["a"])
            for o in range(16):
                sl, sk = load_slab([(lambda s, i=i: v3(s, 2048 * i, 16, 128), w_dn[l, 2048 * i:2048 * (i + 1), o * 128:(o + 1) * 128].rearrange("(k p) n -> p k n", p=128))
                                    for i in range(4)])
                pt, pk = next_ps(0, 4)
                mm16(pt[:, :N], pk, lambda k, sl=sl: v3(sl, 0, 64, 128)[:, k, :], lambda k: a_[:, k, :N], 64, [sk, "a"])
                V(lambda e, o=o, pt=pt: e.scalar_tensor_tensor(out=h[:, o, :N], in0=h[:, o, :N], scalar=ALPHA, in1=pt[:, :N], op0=ALU.mult, op1=ALU.add),
                  ["h", pk], ["h"])
            barrier(["a", "dbgt"], LNK)
            layer_norm(N, 4 + 4 * l, 5 + 4 * l)
            dbg_dump(6, lambda c: h[:, c, :N], 16, N, ["h"])

        sb_rl0 = sb("rl0", [128, NT])
        sb_rl1 = sb("rl1", [128, NT])

        for t in range(n_tiles):
            if t == 0:
                N, c0 = NMETA, 0
                S.dma("sync", lambda e: e.dma_start(out=h[:, :, 0:NMETA], in_=metaT.rearrange("(c p) n -> p c n", p=128)), reads=["hb"], writes=["h"])
            else:
                N, c0 = NT, (t - 1) * NT
                S.dma("sync", lambda e, c0=c0: e.dma_start(out=h[:, :, :], in_=xT[:, c0:c0 + NT].rearrange("(c p) n -> p c n", p=128)), reads=["hb"], writes=["h"])
            dbgcol[0] = 0 if t == 0 else 16 + (t - 1) * NT
            cur_t[0] = t
            barrier(["a", "dbgt"] + SETUP_KEYS, LNK)
            layer_norm(N, 0, 1)
            try:
                chk(1)
                for l in range(n_layers):
                    tile_layer(l, t, N, c0)
            except _Stop:
                break
            if t > 0:
                S.dma("sync", lambda e, c0=c0: e.dma_start(out=outT[:, c0:c0 + NT].rearrange("(c p) n -> p c n", p=128), in_=h[:, :, :]), reads=["h"], writes=["outT"])
        S.emit()
    return nc


def _prep_inputs(inp, b):
    f = np.float32
    oh, g2mask = _host_consts()
    pl = lambda v: np.ascontiguousarray(np.asarray(v, f).reshape(-1, 128).T)
    lnp = np.stack([pl(inp["ln_emb_g"]), pl(inp["ln_emb_b"])] +
                   sum([[pl(inp["ln_mix_g"][l]), pl(inp["ln_mix_b"][l]), pl(inp["ln_mlp_g"][l]), pl(inp["ln_mlp_b"][l])] for l in range(2)], []), axis=1)
    gateb = np.stack([pl(inp["gate_b"][l]) for l in range(2)], axis=1)
    sinks = np.broadcast_to(np.asarray(inp["attn_sinks"], f)[None], (64, 2, 16)).copy()
    relb = np.concatenate([np.asarray(inp["rel_bias"], f), np.full((1, 16), -60.0, f)], axis=0)

    def gn(a):
        return np.asarray(a, f).reshape(32, 2, 64).transpose(1, 2, 0).reshape(128, 32)

    lam = np.stack([np.stack([gn(inp["ssm_lambda_re"][l]), gn(inp["ssm_lambda_im"][l]),
                              gn(np.broadcast_to(np.asarray(inp["ssm_log_step"][l], f)[:, None], (64, 64)))], axis=1) for l in range(2)], axis=1)

    def gb(a):
        return np.asarray(a, f).reshape(32, 2, 64, 16).transpose(1, 2, 0, 3).reshape(128, 32, 16)

    bmat = np.stack([np.stack([gb(inp["ssm_b_re"][l]), gb(inp["ssm_b_im"][l])], axis=1) for l in range(2)], axis=1)

    def gc(a):
        return np.asarray(a, f).reshape(8, 128, 64).transpose(1, 0, 2)

    cmat = np.stack([np.stack([gc(inp["ssm_c_re"][l]), gc(inp["ssm_c_im"][l])], axis=1) for l in range(2)], axis=1)
    dvec = np.stack([np.asarray(inp["ssm_d"][l], f).reshape(8, 128).T for l in range(2)], axis=1)
    m = {
        "xT": np.ascontiguousarray(np.asarray(inp["x"][b], f).T),
        "metaT": np.ascontiguousarray(np.asarray(inp["meta_tokens"], f).T),
        "lnp": np.ascontiguousarray(lnp), "gateb": np.ascontiguousarray(gateb), "sinks": sinks, "relb": relb,
        "oh": oh, "g2mask": g2mask, "lam": np.ascontiguousarray(lam), "bmat": np.ascontiguousarray(bmat),
        "cmat": np.ascontiguousarray(cmat), "dvec": np.ascontiguousarray(dvec),
    }
    for k in ["in_proj", "ssm_w_glu", "w_attn_up", "w_ssm_up", "w_out", "w_mlp_up", "w_mlp_down"]:
        m[k] = np.ascontiguousarray(np.asarray(inp[k], f))
    return m


def kernel(**inputs):
    nc = build_program()
    shared = None
    in_maps = []
    for core in range(8):
        b = core % 4
        m = _prep_inputs(inputs, b) if shared is None else dict(shared, xT=np.ascontiguousarray(np.asarray(inputs["x"][b], np.float32).T))
        if shared is None:
            shared = m
        in_maps.append(m)
    res = run_bass_kernel_spmd(nc, in_maps, core_ids=list(range(8)))
    out = np.stack([np.ascontiguousarray(res.results[b]["outT"].T) for b in range(4)], axis=0)
    return out.astype(np.float32)
```

```python
import contextlib
import math
import numpy as np
import concourse.bass as bass
import concourse.mybir as mybir
from concourse.bass_utils import run_bass_kernel_spmd

F32 = mybir.dt.float32
BF16 = mybir.dt.bfloat16
I32 = mybir.dt.int32
AF = mybir.ActivationFunctionType
ALU = mybir.AluOpType

D = 2048
KC = 16
SEQ = 4096
NMETA = 16
NT = 512
DEPTH = 2
INW = 6656
DFF = 8192
ALPHA = (2 * DEPTH) ** 0.25
EPS = 1e-5
EXPS = [1, 2, 3, 4, 5, 6, 7, 8, 16, 32, 64, 128, 256, 512]
EIDX = {e: i for i, e in enumerate(EXPS)}
NE = len(EXPS)
VARIANT = 0
ENGS = ["tensor", "vector", "scalar", "gpsimd", "sync"]


class _Rec:
    def __init__(self):
        self.calls = []

    def __getattr__(self, name):
        def f(*a, **k):
            self.calls.append((name, a, k))
            return self
        return f


class Sched:
    NDMA = 6

    def __init__(self, nc):
        self.nc = nc
        self.ops = {e: [] for e in ENGS}
        self.last_write = {}
        self.readers = {}

    def _add(self, eng, fn, reads, writes, dma):
        idx = len(self.ops[eng])
        deps = set()
        for k in reads:
            lw = self.last_write.get(k)
            if lw is not None:
                deps.add(lw)
        for k in writes:
            lw = self.last_write.get(k)
            if lw is not None:
                deps.add(lw)
            for r in self.readers.get(k, ()):
                deps.add(r)
        me = (eng, idx)
        deps.discard(me)
        rec = _Rec()
        fn(rec)
        assert len(rec.calls) == 1, rec.calls
        self.ops[eng].append(dict(call=rec.calls[0], deps=deps, sig=False, dma=dma))
        for k in writes:
            self.last_write[k] = me
            self.readers[k] = []
        for k in reads:
            if k not in writes:
                self.readers.setdefault(k, []).append(me)
        return me

    def op(self, eng, fn, reads=(), writes=()):
        return self._add(eng, fn, tuple(reads), tuple(writes), False)

    def dma(self, eng, fn, reads=(), writes=()):
        return self._add(eng, fn, tuple(reads), tuple(writes), True)

    def emit(self):
        nc = self.nc
        ops = self.ops
        for e in ENGS:
            for o in ops[e]:
                for (de, di) in o["deps"]:
                    d = ops[de][di]
                    if not d["dma"] and not (de == "tensor" and e == "tensor"):
                        d["sig"] = True
        for e in ENGS:
            c = 0
            nd = 0
            for o in ops[e]:
                if o["dma"]:
                    o["dslot"] = nd % self.NDMA
                    o["dtarget"] = 16 * (nd // self.NDMA + 1)
                    o["dn"] = nd
                    nd += 1
                elif o["sig"]:
                    c += 1
                    o["cnt"] = c
        with contextlib.ExitStack() as st:
            csem = {e: st.enter_context(nc.semaphore("c_" + e)) for e in ENGS}
            dsem = {e: [st.enter_context(nc.semaphore("d_%s%d" % (e, i))) for i in range(self.NDMA)]
                    for e in ("sync", "scalar", "gpsimd")}
            block = st.enter_context(nc.Block())

            def run(e, eng):
                waited = {}

                def wait(key, sem, val):
                    if waited.get(key, 0) >= val:
                        return
                    eng.wait_ge(sem, val)
                    waited[key] = val

                dma_list = [o for o in ops[e] if o["dma"]]
                for o in ops[e]:
                    for (de, di) in sorted(o["deps"]):
                        d = ops[de][di]
                        if d["dma"]:
                            wait(("d", de, d["dslot"]), dsem[de][d["dslot"]], d["dtarget"])
                        elif not (de == "tensor" and e == "tensor"):
                            wait(("c", de), csem[de], d["cnt"])
                    if o["dma"]:
                        if o["dn"] >= self.NDMA:
                            p = dma_list[o["dn"] - self.NDMA]
                            wait(("d", e, p["dslot"]), dsem[e][p["dslot"]], p["dtarget"])
                        nm, a, k = o["call"]
                        ins = getattr(eng, nm)(*a, **k)
                        ins.then_inc(dsem[e][o["dslot"]], 16)
                    else:
                        nm, a, k = o["call"]
                        ins = getattr(eng, nm)(*a, **k)
                        if o["sig"]:
                            ins.then_inc(csem[e], 1)
                for p in dma_list[-self.NDMA:]:
                    wait(("d", e, p["dslot"]), dsem[e][p["dslot"]], p["dtarget"])

            block.tensor(lambda eng: run("tensor", eng))
            block.vector(lambda eng: run("vector", eng))
            block.scalar(lambda eng: run("scalar", eng))
            block.gpsimd(lambda eng: run("gpsimd", eng))
            block.sync(lambda eng: run("sync", eng))


def _t5_bucket(n):
    n = np.maximum(n, 0)
    nf = np.maximum(n, 1).astype(np.float32)
    large = 16 + (np.log(nf / np.float32(16)) / np.float32(math.log(8.0)) * np.float32(16)).astype(np.int32)
    large = np.minimum(large, 31)
    return np.where(n < 16, n, large)


def _oh_table(dists, valid):
    t = np.zeros((33, len(dists)), np.float32)
    b = _t5_bucket(dists)
    for j in range(len(dists)):
        if valid[j]:
            t[b[j], j] = 1.0
        else:
            t[32, j] = 1.0
    return t


def _host_consts():
    j = np.arange(255)
    d_cur = 127 - j
    oh_cur = _oh_table(d_cur, d_cur >= 0)
    d_prev = 255 - j
    oh_prev = _oh_table(d_prev, d_prev < 128)
    j = np.arange(143)
    oh_m1 = _oh_table(143 - j, np.ones(143, bool))
    j = np.arange(31)
    oh_m0 = _oh_table(15 - j, (15 - j) >= 0)
    oh_mc = np.zeros((33, 16), np.float32)
    oh_mc[31, :] = 1.0
    oh = np.zeros((33, 255 + 255 + 143 + 31 + 16), np.float32)
    oh[:, 0:255] = oh_cur
    oh[:, 255:510] = oh_prev
    oh[:, 510:653] = oh_m1
    oh[:, 653:684] = oh_m0
    oh[:, 684:700] = oh_mc
    p = np.arange(128)
    g2 = (p % 32) // 16
    g2mask = np.stack([(g2 == 0), (g2 == 1)], axis=1).astype(np.float32)
    return oh, g2mask


OH_CUR, OH_PREV, OH_M1, OH_M0, OH_MC = 0, 255, 510, 653, 684


class _Stop(Exception):
    pass


def build_program(n_tiles=9, n_layers=2, debug=False, stop_after=0):
    nc = bass.Bass("TRN2", target_bir_lowering=False)

    def din(name, shape):
        return nc.dram_tensor(name, list(shape), F32, kind="ExternalInput").ap()

    xT = din("xT", [D, SEQ])
    metaT = din("metaT", [D, NMETA])
    lnp_d = din("lnp", [128, 10, 16])
    gateb_d = din("gateb", [128, 2, 32])
    sinks_d = din("sinks", [64, 2, 16])
    relb_d = din("relb", [33, 16])
    oh_d = din("oh", [33, 700])
    g2m_d = din("g2mask", [128, 2])
    lam_d = din("lam", [128, 2, 3, 32])
    bmat_d = din("bmat", [128, 2, 2, 32, 16])
    cmat_d = din("cmat", [128, 2, 2, 8, 64])
    dvec_d = din("dvec", [128, 2, 8])
    w_in = din("in_proj", [2, D, INW])
    w_glu = din("ssm_w_glu", [2, 1024, 1024])
    w_au = din("w_attn_up", [2, 1024, D])
    w_su = din("w_ssm_up", [2, 1024, D])
    w_o = din("w_out", [2, D, D])
    w_up = din("w_mlp_up", [2, D, DFF])
    w_dn = din("w_mlp_down", [2, DFF, D])
    outT = nc.dram_tensor("outT", [D, SEQ], F32, kind="ExternalOutput").ap()
    dbg = None
    if debug:
        dbg = nc.dram_tensor("dbg", [128, 8, 16, 528], F32, kind="ExternalOutput").ap()

    S = Sched(nc)
    with contextlib.ExitStack() as st:
        def sb(name, shape, dt=F32):
            return st.enter_context(nc.sbuf_tensor(name, list(shape), dt))

        psall = st.enter_context(nc.psum_tensor("psall", [128, 8, 512], F32))
        PS = [psall[:, i, :] for i in range(8)]
        PK = ["PS%d" % i for i in range(8)]

        h = sb("h", [128, KC, NT])
        hb = sb("hb", [128, KC, NT], BF16)
        slabs = [sb("slab%d" % i, [128, 8192], BF16) for i in range(2)]
        ident = sb("ident", [128, 128])
        ones = sb("ones", [128, 128])
        ones_bf = sb("ones_bf", [128, 128], BF16)
        halfpi = sb("halfpi", [128, 1])
        lnp = sb("lnp_sb", [128, 10, 16])
        gateb = sb("gateb_sb", [128, 2, 32])
        g2m = sb("g2m_sb", [128, 2])
        dvec = sb("dvec_sb", [128, 2, 8])
        relb = sb("relb_sb", [33, 16])
        E_cur = sb("E_cur", [128, 16, 128])
        E_prev = sb("E_prev", [128, 16, 128])
        E_m1 = sb("E_m1", [128, 16, 128])
        E_m0 = sb("E_m0", [128, 16, 16])
        E_mc = sb("E_mc", [128, 16])
        sinkE = sb("sinkE", [128, 2, 16])
        apr = sb("apr", [128, 2, 32, NE])
        api = sb("api", [128, 2, 32, NE])
        TB = sb("TB", [128, 2, 8, 2, 128], BF16)
        CE = sb("CE", [128, 2, 8, 2, 128], BF16)
        carry_re = sb("carry_re", [128, 2, 32])
        carry_im = sb("carry_im", [128, 2, 32])
        prevK = sb("prevK", [128, 2, 4, 128], BF16)
        prevV = sb("prevV", [128, 2, 4, 128], BF16)
        metaK = sb("metaK", [128, 2, 4, 128], BF16)
        metaV = sb("metaV", [128, 2, 4, 128], BF16)
        UN = sb("union", [128, 16384])

        def uview(off_b, shape, dt):
            n = int(np.prod(shape[1:]))
            esz = 4 if dt == F32 or dt == I32 else 2
            assert off_b % 4 == 0 and off_b + n * esz <= 65536, (off_b, shape)
            if dt == F32:
                v = UN[:shape[0], off_b // 4: off_b // 4 + n]
            else:
                v = UN[:shape[0], off_b // 4: off_b // 4 + (n * esz + 3) // 4].bitcast(dt)[:, 0:n]
            if len(shape) == 2:
                return v
            names = " ".join("d%d" % i for i in range(1, len(shape)))
            kw = {"d%d" % i: shape[i] for i in range(2, len(shape))}
            return v.rearrange("p (%s) -> p %s" % (names, names), **kw)

        KB = 1024
        scr = sb("scr", [128, 8])

        def barrier(old, new):
            S.op("vector", lambda e: e.memset(scr[:, 0:1], 0.0), reads=[], writes=list(old) + list(new))

        LNK = ["lnsq0", "lnsq1", "lnmean", "lnrstd", "lnt0", "lnt1"]
        SSMK = ["u_bf", "yg", "Xre", "Xim", "Hre", "Him", "tt0", "tt1", "tt2", "tt3", "SA0r", "SA0i", "SA1r", "SA1i", "ytmp0", "ytmp1"]
        ATK = ["qT", "kT2", "vt", "yat"]
        ATT = ["ex%d" % i for i in range(6)] + ["pT%d" % i for i in range(6)] + ["rl0", "rl1"]
        GK = ["mixed", "gt0", "gt1", "gt2", "gt3"]
        slab_ctr = [0]

        def load_slab(parts):
            i = slab_ctr[0] % 2
            slab_ctr[0] += 1
            sl = slabs[i]
            key = "slab%d" % i
            for dstf, src in parts:
                S.dma("gpsimd", lambda e, dstf=dstf, src=src, sl=sl: e.dma_start(out=dstf(sl), in_=src), writes=[key])
            return sl, key

        def v3(sl, off, k, n, parts=128):
            return sl[:parts, off:off + k * n].rearrange("p (k n) -> p k n", n=n)

        ps_ctr = [0]

        def next_ps(lo=0, hi=4):
            i = lo + ps_ctr[0] % (hi - lo)
            ps_ctr[0] += 1
            return PS[i], PK[i]

        S.dma("sync", lambda e: e.dma_start(out=lnp[:], in_=lnp_d), writes=["lnp"])
        S.dma("sync", lambda e: e.dma_start(out=gateb[:], in_=gateb_d), writes=["gateb"])
        S.dma("sync", lambda e: e.dma_start(out=g2m[:], in_=g2m_d), writes=["g2m"])
        S.dma("sync", lambda e: e.dma_start(out=dvec[:], in_=dvec_d), writes=["dvec"])
        S.dma("sync", lambda e: e.dma_start(out=relb[:], in_=relb_d), writes=["relb"])
        S.dma("sync", lambda e: e.dma_start(out=sinkE[0:64], in_=sinks_d), writes=["sinkE"])
        S.op("gpsimd", lambda e: e.memset(ident[:], 0.0), writes=["ident"])
        S.op("gpsimd", lambda e: e.affine_select(out=ident[:], in_=ident[:], pattern=[[-1, 128]], compare_op=ALU.not_equal,
                                                 fill=1.0, base=0, channel_multiplier=1), reads=["ident"], writes=["ident"])
        S.op("vector", lambda e: e.memset(ones[:], 1.0), writes=["ones"])
        S.op("vector", lambda e: e.memset(ones_bf[:], 1.0), writes=["ones_bf"])
        S.op("vector", lambda e: e.memset(halfpi[:], math.pi / 2), writes=["halfpi"])
        S.op("vector", lambda e: e.memset(carry_re[:], 0.0), writes=["carry"])
        S.op("vector", lambda e: e.memset(carry_im[:], 0.0), writes=["carry"])
        S.op("scalar", lambda e: e.activation(out=sinkE[0:64], in_=sinkE[0:64], func=AF.Exp), reads=["sinkE"], writes=["sinkE"])
        S.op("gpsimd", lambda e: e.memset(E_m1[:], 0.0), writes=["E"])
        S.op("gpsimd", lambda e: e.memset(E_m0[:], 0.0), writes=["E"])
        S.op("gpsimd", lambda e: e.memset(E_mc[:], 0.0), writes=["E"])
        S.op("gpsimd", lambda e: e.memset(metaK[:], 0.0), writes=["metaK"])
        S.op("gpsimd", lambda e: e.memset(metaV[:], 0.0), writes=["metaV"])

        oh = uview(0, [33, 700], F32)
        S.dma("sync", lambda e: e.dma_start(out=oh, in_=oh_d), writes=["oh"])

        def build_E(Etile, base, off0, K, Q):
            PT = psall[:, 0:4, :].rearrange("p b (q h) -> p (b q) h", h=16)
            for q in range(Q):
                S.op("tensor", lambda e, q=q: e.matmul(PT[0:K, q, :], lhsT=oh[:, base + off0 - q: base + off0 - q + K], rhs=relb[:, :],
                                                       start=True, stop=True),
                     reads=["oh", "relb"], writes=PK[0:4])
            S.op("scalar", lambda e: e.activation(out=Etile[0:K, :, 0:Q], in_=PT[0:K, 0:Q, :].rearrange("p q h -> p h q"), func=AF.Exp),
                 reads=PK[0:4], writes=["E"])

        build_E(E_cur, OH_CUR, 127, 128, 128)
        build_E(E_prev, OH_PREV, 127, 128, 128)
        build_E(E_m1, OH_M1, 127, 16, 128)
        build_E(E_m0, OH_M0, 15, 16, 16)
        S.op("tensor", lambda e: e.matmul(PS[4][0:16, 0:16], lhsT=oh[:, OH_MC:OH_MC + 16], rhs=relb[:, :], start=True, stop=True),
             reads=["oh", "relb"], writes=[PK[4]])
        S.op("scalar", lambda e: e.activation(out=E_mc[0:16, :], in_=PS[4][0:16, 0:16], func=AF.Exp), reads=[PK[4]], writes=["E"])

        if debug:
            S.dma("sync", lambda e: e.dma_start(out=dbg[:, 7, 0, 0:128], in_=E_cur[:, 0, :]), reads=["E"], writes=["dbgE"])
            S.dma("sync", lambda e: e.dma_start(out=dbg[:, 7, 1, 0:128], in_=E_prev[:, 0, :]), reads=["E"], writes=["dbgE"])
            S.dma("sync", lambda e: e.dma_start(out=dbg[:16, 7, 2, 0:128], in_=E_m1[0:16, 0, :]), reads=["E"], writes=["dbgE"])
            S.dma("sync", lambda e: e.dma_start(out=dbg[:, 7, 3, 0:128], in_=E_cur[:, 5, :]), reads=["E"], writes=["dbgE"])
        lam = uview(4 * KB, [128, 2, 3, 32], F32)
        bm = uview(8 * KB, [128, 2, 2, 32, 16], F32)
        cm = uview(16 * KB, [128, 2, 2, 8, 64], F32)
        S.dma("sync", lambda e: e.dma_start(out=lam, in_=lam_d), writes=["lam"])
        S.dma("sync", lambda e: e.dma_start(out=bm, in_=bmat_d), writes=["bm"])
        S.dma("sync", lambda e: e.dma_start(out=cm, in_=cmat_d), writes=["cm"])
        tmp = uview(24 * KB, [128, 16, 32], F32)
        tmpi = uview(26 * KB, [128, 32], I32)
        bbar = uview(28 * KB, [128, 2, 32, 16], F32)
        bexp = uview(32 * KB, [128, 2, 32, 32], F32)
        cexp = uview(40 * KB, [128, 2, 8, 128], F32)
        btmp = uview(48 * KB, [128, 2, 32, 16], F32)

        def V(fn, reads, writes):
            S.op("vector", fn, reads=reads, writes=writes)

        def G(fn, reads, writes):
            S.op("gpsimd", fn, reads=reads, writes=writes)

        for l in range(n_layers):
            lr, li, ls = lam[:, l, 0, :], lam[:, l, 1, :], lam[:, l, 2, :]
            T = lambda i: tmp[:, i, :]
            K_ = ["ssmtmp"]
            S.op("scalar", lambda e, ls=ls: e.activation(out=T(0), in_=ls, func=AF.Exp), reads=["lam"], writes=K_)
            V(lambda e, lr=lr: e.tensor_tensor(out=T(1), in0=lr, in1=T(0), op=ALU.mult), ["lam"] + K_, K_)
            S.op("scalar", lambda e: e.activation(out=T(1), in_=T(1), func=AF.Exp), reads=K_, writes=K_)
            V(lambda e, li=li: e.tensor_tensor(out=T(2), in0=li, in1=T(0), op=ALU.mult), ["lam"] + K_, K_)
            V(lambda e: e.tensor_scalar(out=T(3), in0=T(2), scalar1=1.0 / (2 * math.pi), scalar2=None, op0=ALU.mult), K_, K_)
            V(lambda e: e.tensor_copy(out=tmpi, in_=T(3)), K_, K_)
            V(lambda e: e.tensor_copy(out=T(3), in_=tmpi), K_, K_)
            V(lambda e: e.scalar_tensor_tensor(out=T(2), in0=T(3), scalar=-2 * math.pi, in1=T(2), op0=ALU.mult, op1=ALU.add), K_, K_)
            S.op("scalar", lambda e: e.activation(out=T(4), in_=T(2), func=AF.Sin, scale=0.5), reads=K_, writes=K_)
            S.op("scalar", lambda e: e.activation(out=T(3), in_=T(2), func=AF.Sin, scale=0.25), reads=K_, writes=K_)
            V(lambda e: e.tensor_tensor(out=T(5), in0=T(3), in1=T(3), op=ALU.mult), K_, K_)
            V(lambda e: e.tensor_scalar(out=T(5), in0=T(5), scalar1=-2.0, scalar2=1.0, op0=ALU.mult, op1=ALU.add), K_, K_)
            V(lambda e: e.tensor_tensor(out=T(6), in0=T(4), in1=T(5), op=ALU.mult), K_, K_)
            V(lambda e: e.tensor_scalar(out=T(6), in0=T(6), scalar1=2.0, scalar2=None, op0=ALU.mult), K_, K_)
            V(lambda e: e.tensor_tensor(out=T(7), in0=T(4), in1=T(4), op=ALU.mult), K_, K_)
            V(lambda e: e.tensor_scalar(out=T(7), in0=T(7), scalar1=-2.0, scalar2=1.0, op0=ALU.mult, op1=ALU.add), K_, K_)
            a1r, a1i = apr[:, l, :, 0], api[:, l, :, 0]
            V(lambda e, a1r=a1r: e.tensor_tensor(out=a1r, in0=T(1), in1=T(7), op=ALU.mult), K_, ["apow"])
            V(lambda e, a1i=a1i: e.tensor_tensor(out=a1i, in0=T(1), in1=T(6), op=ALU.mult), K_, ["apow"])

            def cmul(orr, oi, xr, xi, yr, yi):
                V(lambda e: e.tensor_tensor(out=T(8), in0=xr, in1=yr, op=ALU.mult), ["apow"] + K_, K_)
                V(lambda e: e.tensor_tensor(out=T(9), in0=xi, in1=yi, op=ALU.mult), ["apow"] + K_, K_)
                V(lambda e: e.tensor_tensor(out=T(10), in0=xr, in1=yi, op=ALU.mult), ["apow"] + K_, K_)
                V(lambda e: e.tensor_tensor(out=T(11), in0=xi, in1=yr, op=ALU.mult), ["apow"] + K_, K_)
                V(lambda e: e.tensor_tensor(out=orr, in0=T(8), in1=T(9), op=ALU.subtract), K_, ["apow"])
                V(lambda e: e.tensor_tensor(out=oi, in0=T(10), in1=T(11), op=ALU.add), K_, ["apow"])

            for i in range(1, NE):
                e_ = EXPS[i]
                if e_ <= 8:
                    j = EIDX[e_ - 1]
                    cmul(apr[:, l, :, i], api[:, l, :, i], apr[:, l, :, j], api[:, l, :, j], a1r, a1i)
                else:
                    j = EIDX[e_ // 2]
                    cmul(apr[:, l, :, i], api[:, l, :, i], apr[:, l, :, j], api[:, l, :, j], apr[:, l, :, j], api[:, l, :, j])
            V(lambda e, a1r=a1r: e.tensor_scalar(out=T(8), in0=a1r, scalar1=-1.0, scalar2=None, op0=ALU.add), ["apow"] + K_, K_)
            V(lambda e, lr=lr: e.tensor_tensor(out=T(9), in0=lr, in1=lr, op=ALU.mult), ["lam"] + K_, K_)
            V(lambda e, li=li: e.tensor_tensor(out=T(10), in0=li, in1=li, op=ALU.mult), ["lam"] + K_, K_)
            V(lambda e: e.tensor_tensor(out=T(9), in0=T(9), in1=T(10), op=ALU.add), K_, K_)
            V(lambda e: e.reciprocal(out=T(9), in_=T(9)), K_, K_)
            V(lambda e, lr=lr: e.tensor_tensor(out=T(10), in0=T(8), in1=lr, op=ALU.mult), ["lam"] + K_, K_)
            V(lambda e, li=li, a1i=a1i: e.tensor_tensor(out=T(11), in0=a1i, in1=li, op=ALU.mult), ["lam", "apow"] + K_, K_)
            V(lambda e: e.tensor_tensor(out=T(10), in0=T(10), in1=T(11), op=ALU.add), K_, K_)
            V(lambda e: e.tensor_tensor(out=T(12), in0=T(10), in1=T(9), op=ALU.mult), K_, K_)
            V(lambda e, lr=lr, a1i=a1i: e.tensor_tensor(out=T(10), in0=a1i, in1=lr, op=ALU.mult), ["lam", "apow"] + K_, K_)
            V(lambda e, li=li: e.tensor_tensor(out=T(11), in0=T(8), in1=li, op=ALU.mult), ["lam"] + K_, K_)
            V(lambda e: e.tensor_tensor(out=T(10), in0=T(10), in1=T(11), op=ALU.subtract), K_, K_)
            V(lambda e: e.tensor_tensor(out=T(13), in0=T(10), in1=T(9), op=ALU.mult), K_, K_)
            zr_b = T(12).unsqueeze(2).to_broadcast([128, 32, 16])
            zi_b = T(13).unsqueeze(2).to_broadcast([128, 32, 16])
            bre, bim = bm[:, l, 0], bm[:, l, 1]
            V(lambda e: e.tensor_tensor(out=bbar[:, 0], in0=bre, in1=zr_b, op=ALU.mult), ["bm"] + K_, ["bbar"])
            V(lambda e: e.tensor_tensor(out=btmp[:, 0], in0=bim, in1=zi_b, op=ALU.mult), ["bm"] + K_, ["btmp"])
            V(lambda e: e.tensor_tensor(out=bbar[:, 0], in0=bbar[:, 0], in1=btmp[:, 0], op=ALU.subtract), ["bbar", "btmp"], ["bbar"])
            V(lambda e: e.tensor_tensor(out=bbar[:, 1], in0=bim, in1=zr_b, op=ALU.mult), ["bm"] + K_, ["bbar"])
            V(lambda e: e.tensor_tensor(out=btmp[:, 1], in0=bre, in1=zi_b, op=ALU.mult), ["bm"] + K_, ["btmp"])
            V(lambda e: e.tensor_tensor(out=bbar[:, 1], in0=bbar[:, 1], in1=btmp[:, 1], op=ALU.add), ["bbar", "btmp"], ["bbar"])
            V(lambda e: e.memset(bexp, 0.0), [], ["bexp"])
            for ri in range(2):
                V(lambda e, ri=ri: e.tensor_copy(out=bexp[0:64, ri, :, 0:16], in_=bbar[0:64, ri]), ["bbar"], ["bexp"])
                V(lambda e, ri=ri: e.tensor_copy(out=bexp[64:128, ri, :, 16:32], in_=bbar[64:128, ri]), ["bbar"], ["bexp"])
            for ri in range(2):
                sgn = 1.0 if ri == 0 else -1.0
                for g2 in range(2):
                    V(lambda e, ri=ri, g2=g2, sgn=sgn: e.tensor_scalar(out=cexp[:, ri, :, g2 * 64:(g2 + 1) * 64], in0=cm[:, l, ri], scalar1=g2m[:, g2:g2 + 1],
                                                                         scalar2=sgn, op0=ALU.mult, op1=ALU.mult), ["cm", "g2m"], ["cexp"])
            for c in range(8):
                for ri in range(2):
                    pt, pk = next_ps(4, 8)
                    S.op("tensor", lambda e, c=c, ri=ri, pt=pt: e.transpose(out=pt[:, 0:128], in_=bexp[:, ri, 4 * c:4 * c + 4, :].rearrange("p a b -> p (a b)"),
                                                                            identity=ident[:]), reads=["bexp", "ident"], writes=[pk])
                    S.op("scalar", lambda e, c=c, ri=ri, pt=pt: e.activation(out=TB[:, l, c, ri, :], in_=pt[:, 0:128], func=AF.Copy), reads=[pk], writes=["TB"])
                    pt, pk = next_ps(4, 8)
                    S.op("tensor", lambda e, c=c, ri=ri, pt=pt: e.transpose(out=pt[:, 0:128], in_=cexp[:, ri, c, :], identity=ident[:]),
                         reads=["cexp", "ident"], writes=[pk])
                    S.op("scalar", lambda e, c=c, ri=ri, pt=pt: e.activation(out=CE[:, l, c, ri, :], in_=pt[:, 0:128], func=AF.Copy), reads=[pk], writes=["CE"])

        SETUP_KEYS = ["oh", "lam", "bm", "cm", "ssmtmp", "bbar", "bexp", "cexp", "btmp"]

        def layer_norm(N, gi, bi):
            sq = [uview(0, [128, NT], F32), uview(2 * KB, [128, NT], F32)]
            mean = uview(4 * KB, [128, NT], F32)
            rstd = uview(6 * KB, [128, NT], F32)
            t1 = [uview(8 * KB, [128, NT], F32), uview(10 * KB, [128, NT], F32)]
            for c in range(KC):
                s_ = sq[c % 2]
                S.op("scalar", lambda e, c=c, s_=s_: e.activation(out=s_[:, :N], in_=h[:, c, :N], func=AF.Square),
                     reads=["h"], writes=["lnsq%d" % (c % 2)])
                S.op("tensor", lambda e, c=c: e.matmul(PS[4][:, :N], lhsT=ones[:], rhs=h[:, c, :N], start=(c == 0), stop=(c == KC - 1)),
                     reads=["h", "ones"], writes=[PK[4]])
                S.op("tensor", lambda e, c=c, s_=s_: e.matmul(PS[5][:, :N], lhsT=ones[:], rhs=s_[:, :N], start=(c == 0), stop=(c == KC - 1)),
                     reads=["lnsq%d" % (c % 2), "ones"], writes=[PK[5]])
            S.op("scalar", lambda e: e.activation(out=mean[:, :N], in_=PS[4][:, :N], func=AF.Identity, scale=1.0 / D), reads=[PK[4]], writes=["lnmean"])
            V(lambda e: e.tensor_tensor(out=rstd[:, :N], in0=mean[:, :N], in1=mean[:, :N], op=ALU.mult), ["lnmean"], ["lnrstd"])
            V(lambda e: e.scalar_tensor_tensor(out=rstd[:, :N], in0=PS[5][:, :N], scalar=1.0 / D, in1=rstd[:, :N], op0=ALU.mult, op1=ALU.subtract),
              [PK[5], "lnrstd"], ["lnrstd"])
            V(lambda e: e.tensor_scalar(out=rstd[:, :N], in0=rstd[:, :N], scalar1=EPS, scalar2=None, op0=ALU.add), ["lnrstd"], ["lnrstd"])
            S.op("scalar", lambda e: e.activation(out=rstd[:, :N], in_=rstd[:, :N], func=AF.Sqrt), reads=["lnrstd"], writes=["lnrstd"])
            V(lambda e: e.reciprocal(out=rstd[:, :N], in_=rstd[:, :N]), ["lnrstd"], ["lnrstd"])
            for c in range(KC):
                t_ = t1[c % 2]
                tk = "lnt%d" % (c % 2)
                V(lambda e, c=c, t_=t_: e.tensor_tensor(out=t_[:, :N], in0=h[:, c, :N], in1=mean[:, :N], op=ALU.subtract), ["h", "lnmean"], [tk])
                V(lambda e, c=c, t_=t_: e.tensor_tensor(out=t_[:, :N], in0=t_[:, :N], in1=rstd[:, :N], op=ALU.mult), [tk, "lnrstd"], [tk])
                V(lambda e, c=c, t_=t_: e.tensor_scalar(out=h[:, c, :N], in0=t_[:, :N], scalar1=lnp[:, gi, c:c + 1], scalar2=lnp[:, bi, c:c + 1],
                                                        op0=ALU.mult, op1=ALU.add), [tk, "lnp"], ["h"])
                S.op("scalar", lambda e, c=c: e.activation(out=hb[:, c, :N], in_=h[:, c, :N], func=AF.Copy), reads=["h"], writes=["hb"])

        def mm16(out_ps, pk, lhs_fn, rhs_fn, nk, extra_reads):
            for k in range(nk):
                S.op("tensor", lambda e, k=k: e.matmul(out_ps, lhsT=lhs_fn(k), rhs=rhs_fn(k), start=(k == 0), stop=(k == nk - 1)),
                     reads=extra_reads, writes=[pk])

        def dbg_dump(slot, src_fn, nchunks, N, reads, parts=128):
            if not debug:
                return
            for c in range(nchunks):
                t_ = uview(62 * KB, [128, NT], F32)
                V(lambda e, c=c: e.tensor_copy(out=t_[:parts, :N], in_=src_fn(c)), reads + ["dbgt"], ["dbgt"])
                S.dma("sync", lambda e, c=c: e.dma_start(out=dbg[:parts, slot, c, dbgcol[0]:dbgcol[0] + N], in_=t_[:parts, :N]), reads=["dbgt"], writes=["dbgt2"])

        dbgcol = [0]

        cur_t = [0]

        def chk(k):
            if stop_after == k and cur_t[0] == n_tiles - 1:
                raise _Stop()

        def tile_layer(l, t, N, c0):
            nblk = max(1, N // 128)
            nch = N // 8
            L = nch + 1
            u_bf = uview(0, [128, 8, NT], BF16)
            yg = uview(8 * KB, [128, 8, NT], BF16)
            Xre = uview(16 * KB, [128, 4, NT], F32)
            Xim = uview(24 * KB, [128, 4, NT], F32)
            Hre = uview(32 * KB, [128, 4, NT], BF16)
            Him = uview(36 * KB, [128, 4, NT], BF16)
            tt = [uview(40 * KB + i * KB, [128, 4, 64], F32) for i in range(4)]
            SA = [[uview(44 * KB + (2 * a + b) * 1040, [128, 4, 65], F32) for b in range(2)] for a in range(2)]
            ytmp = [uview(50 * KB, [128, NT], F32), uview(52 * KB, [128, NT], F32)]
            yssm = uview(54 * KB, [128, 8, NT], BF16)
            barrier(["a", "dbgt"] + LNK + SETUP_KEYS, SSMK + ["yssm"])
            for j in range(2):
                sl, sk = load_slab([(lambda s: v3(s, 0, 16, 512), w_in[l, :, 1536 + j * 512: 1536 + (j + 1) * 512].rearrange("(k p) n -> p k n", p=128))])
                for cc in range(4):
                    c = 4 * j + cc
                    pt, pk = next_ps(0, 4)
                    mm16(pt[:, :N], pk, lambda k, sl=sl, cc=cc: v3(sl, 0, 16, 512)[:, k, cc * 128:(cc + 1) * 128], lambda k: hb[:, k, :N], KC, [sk, "hb"])
                    S.op("scalar", lambda e, c=c, pt=pt: e.activation(out=u_bf[:, c, :N], in_=pt[:, :N], func=AF.Copy), reads=[pk], writes=["u_bf"])
            dbg_dump(0, lambda c: u_bf[:, c, :N], 8, N, ["u_bf"])
            chk(2)
            for c in range(8):
                for q in range(4):
                    for ri, Xt, xk in ((0, Xre, "Xre"), (1, Xim, "Xim")):
                        pt, pk = next_ps(0, 4)
                        S.op("tensor", lambda e, q=q, ri=ri, pt=pt, c=c: e.matmul(pt[:, :N], lhsT=TB[32 * q:32 * q + 32, l, c, ri, :], rhs=u_bf[32 * q:32 * q + 32, c, :N],
                                                                                    start=True, stop=True, tile_position=(32 * q, 0)),
                             reads=["TB", "u_bf"], writes=[pk])
                        S.op("scalar", lambda e, q=q, pt=pt, Xt=Xt: e.activation(out=Xt[:, q, :N], in_=pt[:, :N], func=AF.Copy), reads=[pk], writes=[xk])
                Xr4 = Xre[:, :, :N].rearrange("p g (c j) -> p g c j", j=8)
                Xi4 = Xim[:, :, :N].rearrange("p g (c j) -> p g c j", j=8)
                Hr4 = Hre[:, :, :N].rearrange("p g (c j) -> p g c j", j=8)
                Hi4 = Him[:, :, :N].rearrange("p g (c j) -> p g c j", j=8)

                def ab(e_, which, n_):
                    src = apr if which == 0 else api
                    return src[:, l, 4 * c:4 * c + 4, EIDX[e_]].unsqueeze(2).to_broadcast([128, 4, n_])

                T0_, T1_, T2_, T3_ = [x[:, :, :nch] for x in tt]
                TK = ["tt0", "tt1", "tt2", "tt3"]
                for j in range(1, 8):
                    V(lambda e, j=j: e.tensor_tensor(out=T0_, in0=Xr4[:, :, :, j - 1], in1=ab(1, 0, nch), op=ALU.mult), ["Xre", "apow"], [TK[0]])
                    V(lambda e, j=j: e.tensor_tensor(out=T1_, in0=Xi4[:, :, :, j - 1], in1=ab(1, 1, nch), op=ALU.mult), ["Xim", "apow"], [TK[1]])
                    G(lambda e, j=j: e.tensor_tensor(out=T2_, in0=Xi4[:, :, :, j - 1], in1=ab(1, 0, nch), op=ALU.mult), ["Xim", "apow"], [TK[2]])
                    G(lambda e, j=j: e.tensor_tensor(out=T3_, in0=Xr4[:, :, :, j - 1], in1=ab(1, 1, nch), op=ALU.mult), ["Xre", "apow"], [TK[3]])
                    V(lambda e, j=j: e.tensor_tensor(out=Xr4[:, :, :, j], in0=Xr4[:, :, :, j], in1=T0_, op=ALU.add), ["Xre", TK[0]], ["Xre"])
                    V(lambda e, j=j: e.tensor_tensor(out=Xr4[:, :, :, j], in0=Xr4[:, :, :, j], in1=T1_, op=ALU.subtract), ["Xre", TK[1]], ["Xre"])
                    G(lambda e, j=j: e.tensor_tensor(out=Xi4[:, :, :, j], in0=Xi4[:, :, :, j], in1=T2_, op=ALU.add), ["Xim", TK[2]], ["Xim"])
                    G(lambda e, j=j: e.tensor_tensor(out=Xi4[:, :, :, j], in0=Xi4[:, :, :, j], in1=T3_, op=ALU.add), ["Xim", TK[3]], ["Xim"])
                A, B = SA[0], SA[1]
                AK, BK = ["SA0r", "SA0i"], ["SA1r", "SA1i"]
                V(lambda e: e.tensor_copy(out=A[0][:, :, 0], in_=carry_re[:, l, 4 * c:4 * c + 4]), ["carry"], [AK[0]])
                G(lambda e: e.tensor_copy(out=A[1][:, :, 0], in_=carry_im[:, l, 4 * c:4 * c + 4]), ["carry"], [AK[1]])
                V(lambda e: e.tensor_copy(out=A[0][:, :, 1:L], in_=Xr4[:, :, :, 7]), ["Xre"], [AK[0]])
                G(lambda e: e.tensor_copy(out=A[1][:, :, 1:L], in_=Xi4[:, :, :, 7]), ["Xim"], [AK[1]])
                d = 1
                while d < L:
                    m = L - d
                    e8 = 8 * d
                    V(lambda e, A=A, B=B, d=d: e.tensor_copy(out=B[0][:, :, 0:d], in_=A[0][:, :, 0:d]), [AK[0]], [BK[0]])
                    G(lambda e, A=A, B=B, d=d: e.tensor_copy(out=B[1][:, :, 0:d], in_=A[1][:, :, 0:d]), [AK[1]], [BK[1]])
                    V(lambda e, A=A, m=m, e8=e8: e.tensor_tensor(out=tt[0][:, :, :m], in0=A[0][:, :, 0:m], in1=ab(e8, 0, m), op=ALU.mult), [AK[0], "apow"], [TK[0]])
                    V(lambda e, A=A, m=m, e8=e8: e.tensor_tensor(out=tt[1][:, :, :m], in0=A[1][:, :, 0:m], in1=ab(e8, 1, m), op=ALU.mult), [AK[1], "apow"], [TK[1]])
                    G(lambda e, A=A, m=m, e8=e8: e.tensor_tensor(out=tt[2][:, :, :m], in0=A[1][:, :, 0:m], in1=ab(e8, 0, m), op=ALU.mult), [AK[1], "apow"], [TK[2]])
                    G(lambda e, A=A, m=m, e8=e8: e.tensor_tensor(out=tt[3][:, :, :m], in0=A[0][:, :, 0:m], in1=ab(e8, 1, m), op=ALU.mult), [AK[0], "apow"], [TK[3]])
                    V(lambda e, A=A, B=B, d=d, m=m: e.tensor_tensor(out=B[0][:, :, d:L], in0=A[0][:, :, d:L], in1=tt[0][:, :, :m], op=ALU.add), [AK[0], TK[0]], [BK[0]])
                    V(lambda e, B=B, d=d, m=m: e.tensor_tensor(out=B[0][:, :, d:L], in0=B[0][:, :, d:L], in1=tt[1][:, :, :m], op=ALU.subtract), [BK[0], TK[1]], [BK[0]])
                    G(lambda e, A=A, B=B, d=d, m=m: e.tensor_tensor(out=B[1][:, :, d:L], in0=A[1][:, :, d:L], in1=tt[2][:, :, :m], op=ALU.add), [AK[1], TK[2]], [BK[1]])
                    G(lambda e, B=B, d=d, m=m: e.tensor_tensor(out=B[1][:, :, d:L], in0=B[1][:, :, d:L], in1=tt[3][:, :, :m], op=ALU.add), [BK[1], TK[3]], [BK[1]])
                    A, B = B, A
                    AK, BK = BK, AK
                    d *= 2
                V(lambda e, A=A: e.tensor_copy(out=carry_re[:, l, 4 * c:4 * c + 4], in_=A[0][:, :, nch]), [AK[0]], ["carry"])
                V(lambda e, A=A: e.tensor_copy(out=carry_im[:, l, 4 * c:4 * c + 4], in_=A[1][:, :, nch]), [AK[1]], ["carry"])
                for j in range(8):
                    V(lambda e, A=A, j=j: e.tensor_tensor(out=T0_, in0=A[0][:, :, 0:nch], in1=ab(j + 1, 0, nch), op=ALU.mult), [AK[0], "apow"], [TK[0]])
                    V(lambda e, A=A, j=j: e.tensor_tensor(out=T1_, in0=A[1][:, :, 0:nch], in1=ab(j + 1, 1, nch), op=ALU.mult), [AK[1], "apow"], [TK[1]])
                    G(lambda e, A=A, j=j: e.tensor_tensor(out=T2_, in0=A[1][:, :, 0:nch], in1=ab(j + 1, 0, nch), op=ALU.mult), [AK[1], "apow"], [TK[2]])
                    G(lambda e, A=A, j=j: e.tensor_tensor(out=T3_, in0=A[0][:, :, 0:nch], in1=ab(j + 1, 1, nch), op=ALU.mult), [AK[0], "apow"], [TK[3]])
                    V(lambda e: e.tensor_tensor(out=T0_, in0=T0_, in1=T1_, op=ALU.subtract), [TK[0], TK[1]], [TK[0]])
                    V(lambda e, j=j: e.tensor_tensor(out=Hr4[:, :, :, j], in0=Xr4[:, :, :, j], in1=T0_, op=ALU.add), ["Xre", TK[0]], ["Hre"])
                    G(lambda e: e.tensor_tensor(out=T2_, in0=T2_, in1=T3_, op=ALU.add), [TK[2], TK[3]], [TK[2]])
                    G(lambda e, j=j: e.tensor_tensor(out=Hi4[:, :, :, j], in0=Xi4[:, :, :, j], in1=T2_, op=ALU.add), ["Xim", TK[2]], ["Him"])
                pt, pk = next_ps(4, 8)
                for q in range(4):
                    S.op("tensor", lambda e, q=q, pt=pt, c=c: e.matmul(pt[32 * q:32 * q + 32, :N], lhsT=CE[:, l, c, 0, 32 * q:32 * q + 32], rhs=Hre[:, q, :N],
                                                                        start=True, stop=False, tile_position=(0, 32 * q)), reads=["CE", "Hre"], writes=[pk])
                    S.op("tensor", lambda e, q=q, pt=pt, c=c: e.matmul(pt[32 * q:32 * q + 32, :N], lhsT=CE[:, l, c, 1, 32 * q:32 * q + 32], rhs=Him[:, q, :N],
                                                                        start=False, stop=True, tile_position=(0, 32 * q)), reads=["CE", "Him"], writes=[pk])
                yt = ytmp[c % 2]
                yk = "ytmp%d" % (c % 2)
                V(lambda e, c=c, pt=pt, yt=yt: e.scalar_tensor_tensor(out=yt[:, :N], in0=u_bf[:, c, :N], scalar=dvec[:, l, c:c + 1], in1=pt[:, :N], op0=ALU.mult, op1=ALU.add),
                  ["u_bf", "dvec", pk], [yk])
                S.op("scalar", lambda e, c=c, yt=yt: e.activation(out=yg[:, c, :N], in_=yt[:, :N], func=AF.Gelu_apprx_tanh), reads=[yk], writes=["yg"])
            chk(3)
            sl, sk = load_slab([(lambda s: v3(s, 0, 8, 1024), w_glu[l].rearrange("(k p) n -> p k n", p=128))])
            for oc in range(8):
                pt, pk = next_ps(0, 4)
                mm16(pt[:, :N], pk, lambda k, sl=sl, oc=oc: v3(sl, 0, 8, 1024)[:, k, oc * 128:(oc + 1) * 128], lambda k: yg[:, k, :N], 8, [sk, "yg"])
                yt = ytmp[oc % 2]
                yk = "ytmp%d" % (oc % 2)
                S.op("scalar", lambda e, pt=pt, yt=yt: e.activation(out=yt[:, :N], in_=pt[:, :N], func=AF.Sigmoid), reads=[pk], writes=[yk])
                V(lambda e, oc=oc, yt=yt: e.tensor_tensor(out=yssm[:, oc, :N], in0=yg[:, oc, :N], in1=yt[:, :N], op=ALU.mult), ["yg", yk], ["yssm"])
            dbg_dump(1, lambda c: yssm[:, c, :N], 8, N, ["yssm"])
            chk(4)

            qT = uview(0, [128, 16, NT], BF16)
            kT2 = uview(16 * KB, [128, 4, NT], BF16)
            vt = uview(20 * KB, [128, 4, 4, 128], BF16)
            yat = uview(24 * KB, [64, 16, NT], BF16)
            ex = [uview(40 * KB + (i % 4) * 2 * KB, [128, NT], F32) for i in range(6)]
            pT = [uview(48 * KB + (i % 4) * KB, [128, NT], BF16) for i in range(6)]
            rd = [sb_rl0[0:64, :], sb_rl1[0:64, :]]
            barrier(SSMK + ["dbgt", "rl0", "rl1"], ATK + ATT)
            S.op("gpsimd", lambda e: e.memset(qT, 0.0), writes=["qT"])
            for j in range(2):
                sl, sk = load_slab([(lambda s: v3(s, 0, 16, 512), w_in[l, :, j * 512:(j + 1) * 512].rearrange("(k p) n -> p k n", p=128))])
                for cc in range(4):
                    c = 4 * j + cc
                    pt, pk = next_ps(0, 4)
                    mm16(pt[:, :N], pk, lambda k, sl=sl, cc=cc: v3(sl, 0, 16, 512)[:, k, cc * 128:(cc + 1) * 128], lambda k: hb[:, k, :N], KC, [sk, "hb"])
                    S.op("scalar", lambda e, c=c, pt=pt: e.activation(out=qT[0:64, 2 * c, :N], in_=pt[0:64, :N], func=AF.Copy), reads=[pk], writes=["qT"])
                    S.op("scalar", lambda e, c=c, pt=pt: e.activation(out=qT[64:128, 2 * c + 1, :N], in_=pt[64:128, :N], func=AF.Copy), reads=[pk], writes=["qT"])
            S.op("gpsimd", lambda e: e.memset(vt, 0.0), writes=["vt"])
            sl, sk = load_slab([(lambda s: v3(s, 0, 16, 512), w_in[l, :, 1024:1536].rearrange("(k p) n -> p k n", p=128))])
            for kvh in range(4):
                pt, pk = next_ps(0, 4)
                for half in range(2):
                    for k in range(KC):
                        S.op("tensor", lambda e, k=k, half=half, kvh=kvh, pt=pt, sl=sl: e.matmul(pt[64 * half:64 * half + 64, :N], lhsT=v3(sl, 0, 16, 512)[:, k, kvh * 64:(kvh + 1) * 64],
                                                                                              rhs=hb[:, k, :N], start=(k == 0), stop=(k == KC - 1)),
                             reads=[sk, "hb"], writes=[pk])
                S.op("scalar", lambda e, kvh=kvh, pt=pt: e.activation(out=kT2[:, kvh, :N], in_=pt[:, :N], func=AF.Copy), reads=[pk], writes=["kT2"])
            nb_tok = min(N, 128)
            for b in range(nblk):
                pt, pk = next_ps(0, 4)
                mm16(pt[:nb_tok, 0:256], pk, lambda k, b=b: hb[:, k, b * 128:b * 128 + nb_tok], lambda k, sl=sl: v3(sl, 0, 16, 512)[:, k, 256:512], KC, [sk, "hb"])
                S.op("scalar", lambda e, b=b, pt=pt: e.activation(out=vt[:nb_tok, b, :, 0:64], in_=pt[:nb_tok, 0:256].rearrange("p (k d) -> p k d", d=64), func=AF.Copy),
                     reads=[pk], writes=["vt"])
            if t == 0:
                S.op("vector", lambda e: e.tensor_copy(out=metaK[:, l, :, 0:16], in_=kT2[:, :, 0:16]), reads=["kT2"], writes=["metaK"])
                S.op("vector", lambda e: e.tensor_copy(out=metaV[0:16, l, :, :], in_=vt[0:16, 0, :, :]), reads=["vt"], writes=["metaV"])
            dbg_dump(2, lambda c: qT[:, c, :N], 8, N, ["qT"])
            chk(5)
            ei = [0]
            if VARIANT == 30 and t == 0:
                S.op("gpsimd", lambda e: e.memset(yat[:, :, :N], 0.0), writes=["yat"])
            for b in range(nblk if not (VARIANT == 30 and t == 0) else 0):
                cols = slice(b * 128, b * 128 + nb_tok)
                gb = (t - 1) * 4 + b + 1 if t > 0 else 0
                for kvh_i in range(4):
                    kvh = 0 if (VARIANT == 20) else kvh_i
                    segs = []
                    if t == 0:
                        segs.append(("m0", 128, lambda half, kvh=kvh: metaK[:, l, kvh, :],
                                     lambda kvh=kvh: metaV[:, l, kvh, :], ["metaK", "metaV"]))
                    else:
                        segs.append(("m1" if gb == 1 else "mc", 128, lambda half, kvh=kvh: metaK[:, l, kvh, :],
                                     lambda kvh=kvh: metaV[:, l, kvh, :], ["metaK", "metaV"]))
                        if gb > 1:
                            if b == 0:
                                segs.append(("prev", 128, lambda half, kvh=kvh: prevK[:, l, kvh, :],
                                             lambda kvh=kvh: prevV[:, l, kvh, :], ["prevK", "prevV"]))
                            else:
                                segs.append(("prev", 128, lambda half, kvh=kvh, b=b: kT2[:, kvh, (b - 1) * 128:b * 128],
                                             lambda kvh=kvh, b=b: vt[:, b - 1, kvh, :], ["kT2", "vt"]))
                        segs.append(("cur", 128, lambda half, kvh=kvh, b=b: kT2[:, kvh, b * 128:(b + 1) * 128],
                                     lambda kvh=kvh, b=b: vt[:, b, kvh, :], ["kT2", "vt"]))
                    W_ = 4 * nb_tok
                    po, pok = PS[6], PK[6]
                    pd, pdk = PS[7], PK[7]
                    nseg = len(segs)
                    pts = []
                    for si, (kind, nk, kf, vf, rk) in enumerate(segs):
                        sps, spk = next_ps(0, 1) if VARIANT in (10, 12) else next_ps(0, 6)
                        for g in range(4):
                            hh = 4 * kvh + g
                            S.op("tensor", lambda e, g=g, hh=hh, kf=kf, sps=sps, nk=nk, cols=cols: e.matmul(
                                sps[0:nk, g * nb_tok:(g + 1) * nb_tok], lhsT=kf(2), rhs=qT[:, hh, cols], start=True, stop=True),
                                 reads=rk + ["qT"], writes=[spk])
                        chk(60)
                        if kvh_i == 1:
                            chk(70)
                        xi = ei[0] % 4
                        ei[0] += 1
                        S.op("scalar", lambda e, sps=sps, nk=nk, xi=xi: e.activation(out=ex[xi][0:nk, :W_], in_=sps[0:nk, :W_], func=AF.Exp, scale=0.125),
                             reads=[spk], writes=["ex%d" % xi])
                        chk(61)
                        if kvh == 1:
                            chk(71)
                        if kind == "mc":
                            Et = E_mc[:, 4 * kvh:4 * kvh + 4].unsqueeze(2).to_broadcast([128, 4, nb_tok])
                            S.op("gpsimd", lambda e, xi=xi, nk=nk, Et=Et: e.tensor_tensor(out=pT[xi][0:nk, :W_].rearrange("p (g q) -> p g q", g=4),
                                                                               in0=ex[xi][0:nk, :W_].rearrange("p (g q) -> p g q", g=4), in1=Et, op=ALU.mult),
                                 reads=["ex%d" % xi, "E"], writes=["pT%d" % xi])
                        else:
                            Esrc = {"cur": E_cur, "prev": E_prev, "m1": E_m1, "m0": E_m0}[kind]
                            Et = Esrc[0:nk, 4 * kvh:4 * kvh + 4, :].rearrange("p g q -> p (g q)")
                            if VARIANT == 1:
                                Et = ex[xi][0:nk, :W_]
                            outp = pT[xi][0:nk, :W_] if VARIANT != 3 else rd[0][0:nk, :W_]
                            S.op("gpsimd", lambda e, xi=xi, nk=nk, Et=Et, outp=outp: e.tensor_tensor(out=outp, in0=ex[xi][0:nk, :W_], in1=Et, op=ALU.mult),
                                 reads=["ex%d" % xi, "E"], writes=["pT%d" % xi])
                        pts.append((xi, nk, vf, rk))
                        chk(62)
                        if kvh == 1:
                            chk(72)
                    if VARIANT == 41:
                        continue
                    for si, (xi, nk, vf, rk) in enumerate(pts):
                        S.op("tensor", lambda e, xi=xi, nk=nk, vf=vf, si=si: e.matmul(po[:, :W_], lhsT=vf(), rhs=pT[xi][0:nk, :W_], start=(si == 0), stop=(si == nseg - 1)),
                             reads=rk + ["pT%d" % xi], writes=[pok])
                    for si, (xi, nk, vf, rk) in enumerate(pts):
                        S.op("tensor", lambda e, xi=xi, nk=nk, si=si: e.matmul(pd[:, :W_], lhsT=ones_bf[0:nk, :], rhs=pT[xi][0:nk, :W_], start=(si == 0), stop=(si == nseg - 1)),
                             reads=["ones_bf", "pT%d" % xi], writes=[pdk])
                    if VARIANT in (42, 43):
                        S.op("tensor", lambda e: e.matmul(pd[:, 256:384], lhsT=hb[:, 0, 0:128] if VARIANT == 43 else metaK[:, l, 0, :], rhs=metaK[:, l, 1, :], start=True, stop=True),
                             reads=["metaK", "hb"], writes=[pdk])
                    chk(63)
                    if kvh == 1:
                        chk(73)
                    if VARIANT in (40, 42, 43):
                        continue
                    r_, o_ = rd[0], rd[1]
                    V(lambda e, r_=r_, kvh=kvh: e.tensor_tensor(out=r_[:, :W_].rearrange("p (g q) -> p g q", g=4), in0=pd[0:64, :W_].rearrange("p (g q) -> p g q", g=4),
                                                                in1=sinkE[0:64, l, 4 * kvh:4 * kvh + 4].unsqueeze(2).to_broadcast([64, 4, nb_tok]), op=ALU.add),
                      [pdk, "sinkE"], ["rl0"])
                    chk(64)
                    if kvh == 1:
                        chk(74)
                    V(lambda e, o_=o_: e.tensor_copy(out=o_[:, :W_], in_=po[0:64, :W_]), [pok], ["rl1"])
                    chk(65)
                    if kvh == 1:
                        chk(75)
                    S.op("scalar", lambda e, r_=r_: e.activation(out=r_[:, :W_], in_=r_[:, :W_], func=AF.Ln), reads=["rl0"], writes=["rl0"])
                    S.op("scalar", lambda e, r_=r_: e.activation(out=r_[:, :W_], in_=r_[:, :W_], func=AF.Exp, scale=-1.0), reads=["rl0"], writes=["rl0"])
                    chk(66)
                    if kvh == 1:
                        chk(76)
                    S.op("gpsimd", lambda e, r_=r_, o_=o_, kvh=kvh, cols=cols: e.tensor_tensor(out=yat[:, 4 * kvh:4 * kvh + 4, cols], in0=o_[:, :W_].rearrange("p (g q) -> p g q", g=4),
                                                                                      in1=r_[:, :W_].rearrange("p (g q) -> p g q", g=4), op=ALU.mult),
                         reads=["rl0", "rl1"], writes=["yat"])
                    chk(67 + kvh)
            if t > 0:
                S.op("vector", lambda e: e.tensor_copy(out=prevK[:, l, :, :], in_=kT2[:, :, 384:512]), reads=["kT2"], writes=["prevK"])
                S.op("vector", lambda e: e.tensor_copy(out=prevV[:, l, :, :], in_=vt[:, 3, :, :]), reads=["vt"], writes=["prevV"])
            if VARIANT == 7:
                dbg_dump(3, lambda c: yat[:, c, :N], 16, N, ["yat"], parts=64)
            chk(6)

            mixed = uview(0, [128, 16, NT], BF16)
            gt = [uview(40 * KB + (i % 2) * 2 * KB, [128, NT], F32) for i in range(4)]
            barrier(ATT + ["dbgt", "qT"], GK)
            for o in range(16):
                sl, sk = load_slab([
                    (lambda s: v3(s, 0, 16, 128), w_in[l, :, 2560 + o * 128: 2560 + (o + 1) * 128].rearrange("(k p) n -> p k n", p=128)),
                    (lambda s: v3(s, 2048, 16, 128), w_in[l, :, 4608 + o * 128: 4608 + (o + 1) * 128].rearrange("(k p) n -> p k n", p=128)),
                    (lambda s: v3(s, 4096, 16, 128, parts=64), w_au[l, :, o * 128:(o + 1) * 128].rearrange("(h d) n -> d h n", d=64)),
                    (lambda s: v3(s, 6144, 8, 128), w_su[l, :, o * 128:(o + 1) * 128].rearrange("(k p) n -> p k n", p=128)),
                ])
                base = 0 if o % 2 == 0 else 4
                pa, pb_, pc, pd_ = PS[base], PS[base + 1], PS[base + 2], PS[base + 3]
                ka, kb, kc_, kd = PK[base], PK[base + 1], PK[base + 2], PK[base + 3]
                mm16(pa[:, :N], ka, lambda k, sl=sl: v3(sl, 0, 16, 128)[:, k, :], lambda k: hb[:, k, :N], 16, [sk, "hb"])
                mm16(pb_[:, :N], kb, lambda k, sl=sl: v3(sl, 2048, 16, 128)[:, k, :], lambda k: hb[:, k, :N], 16, [sk, "hb"])
                mm16(pc[:, :N], kc_, lambda k, sl=sl: v3(sl, 4096, 16, 128, parts=64)[:, k, :], lambda k: yat[:, k, :N], 16, [sk, "yat"])
                mm16(pd_[:, :N], kd, lambda k, sl=sl: v3(sl, 6144, 8, 128)[:, k, :], lambda k: yssm[:, k, :N], 8, [sk, "yssm"])
                i0 = (o % 2) * 2
                g0, g1 = gt[i0], gt[i0 + 1]
                gk0, gk1 = "gt0", "gt1"
                S.op("scalar", lambda e, o=o, pa=pa, g0=g0: e.activation(out=g0[:, :N], in_=pa[:, :N], func=AF.Sigmoid, bias=gateb[:, l, o:o + 1], scale=1.0),
                     reads=[ka, "gateb"], writes=[gk0])
                S.op("scalar", lambda e, o=o, pb_=pb_, g1=g1: e.activation(out=g1[:, :N], in_=pb_[:, :N], func=AF.Sigmoid, bias=gateb[:, l, 16 + o:17 + o], scale=1.0),
                     reads=[kb, "gateb"], writes=[gk1])
                V(lambda e, g0=g0, pc=pc: e.tensor_tensor(out=g0[:, :N], in0=g0[:, :N], in1=pc[:, :N], op=ALU.mult), [gk0, kc_], [gk0])
                V(lambda e, g1=g1, pd_=pd_: e.tensor_tensor(out=g1[:, :N], in0=g1[:, :N], in1=pd_[:, :N], op=ALU.mult), [gk1, kd], [gk1])
                V(lambda e, o=o, g0=g0, g1=g1: e.tensor_tensor(out=mixed[:, o, :N], in0=g0[:, :N], in1=g1[:, :N], op=ALU.add), [gk0, gk1], ["mixed"])
            dbg_dump(4, lambda c: mixed[:, c, :N], 16, N, ["mixed"])
            chk(7)

            for j in range(4):
                sl, sk = load_slab([(lambda s: v3(s, 0, 16, 512), w_o[l, :, j * 512:(j + 1) * 512].rearrange("(k p) n -> p k n", p=128))])
                for cc in range(4):
                    o = 4 * j + cc
                    pt, pk = next_ps(0, 4)
                    mm16(pt[:, :N], pk, lambda k, sl=sl, cc=cc: v3(sl, 0, 16, 512)[:, k, cc * 128:(cc + 1) * 128], lambda k: mixed[:, k, :N], 16, [sk, "mixed"])
                    V(lambda e, o=o, pt=pt: e.scalar_tensor_tensor(out=h[:, o, :N], in0=h[:, o, :N], scalar=ALPHA, in1=pt[:, :N], op0=ALU.mult, op1=ALU.add),
                      ["h", pk], ["h"])
            barrier(ATK + ["dbgt"], LNK)
            layer_norm(N, 2 + 4 * l, 3 + 4 * l)
            dbg_dump(5, lambda c: h[:, c, :N], 16, N, ["h"])
            chk(8)

            a_ = uview(0, [128, 64, NT], BF16)
            rl = [sb_rl0, sb_rl1]
            barrier(LNK + ATK + ATT + GK + SSMK + ["yssm", "dbgt"], ["a"])
            for j in range(16):
                sl, sk = load_slab([(lambda s: v3(s, 0, 16, 512), w_up[l, :, j * 512:(j + 1) * 512].rearrange("(k p) n -> p k n", p=128))])
                for cc in range(4):
                    f = 4 * j + cc
                    pt, pk = next_ps(0, 4)
                    mm16(pt[:, :N], pk, lambda k, sl=sl, cc=cc: v3(sl, 0, 16, 512)[:, k, cc * 128:(cc + 1) * 128], lambda k: hb[:, k, :N], 16, [sk, "hb"])
                    r_ = rl[f % 2]
                    S.op("scalar", lambda e, pt=pt, r_=r_: e.activation(out=r_[:, :N], in_=pt[:, :N], func=AF.Relu), reads=[pk], writes=["rl%d" % (f % 2)])
                    V(lambda e, f=f, r_=r_: e.tensor_tensor(out=a_[:, f, :N], in0=r_[:, :N], in1=r_[:, :N], op=ALU.mult), ["rl%d" % (f % 2)], ["a"])
            for o in range(16):
                sl, sk = load_slab([(lambda s, i=i: v3(s, 2048 * i, 16, 128), w_dn[l, 2048 * i:2048 * (i + 1), o * 128:(o + 1) * 128].rearrange("(k p) n -> p k n", p=128))
                                    for i in range(4)])
                pt, pk = next_ps(0, 4)
                mm16(pt[:, :N], pk, lambda k, sl=sl: v3(sl, 0, 64, 128)[:, k, :], lambda k: a_[:, k, :N], 64, [sk, "a"])
                V(lambda e, o=o, pt=pt: e.scalar_tensor_tensor(out=h[:, o, :N], in0=h[:, o, :N], scalar=ALPHA, in1=pt[:, :N], op0=ALU.mult, op1=ALU.add),
                  ["h", pk], ["h"])
            barrier(["a", "dbgt"], LNK)
            layer_norm(N, 4 + 4 * l, 5 + 4 * l)
            dbg_dump(6, lambda c: h[:, c, :N], 16, N, ["h"])

        sb_rl0 = sb("rl0", [128, NT])
        sb_rl1 = sb("rl1", [128, NT])

        for t in range(n_tiles):
            if t == 0:
                N, c0 = NMETA, 0
                S.dma("sync", lambda e: e.dma_start(out=h[:, :, 0:NMETA], in_=metaT.rearrange("(c p) n -> p c n", p=128)), reads=["hb"], writes=["h"])
            else:
                N, c0 = NT, (t - 1) * NT
                S.dma("sync", lambda e, c0=c0: e.dma_start(out=h[:, :, :], in_=xT[:, c0:c0 + NT].rearrange("(c p) n -> p c n", p=128)), reads=["hb"], writes=["h"])
            dbgcol[0] = 0 if t == 0 else 16 + (t - 1) * NT
            cur_t[0] = t
            barrier(["a", "dbgt"] + SETUP_KEYS, LNK)
            layer_norm(N, 0, 1)
            try:
                chk(1)
                for l in range(n_layers):
                    tile_layer(l, t, N, c0)
            except _Stop:
                break
            if t > 0:
                S.dma("sync", lambda e, c0=c0: e.dma_start(out=outT[:, c0:c0 + NT].rearrange("(c p) n -> p c n", p=128), in_=h[:, :, :]), reads=["h"], writes=["outT"])
        S.emit()
    return nc


def _prep_inputs(inp, b):
    f = np.float32
    oh, g2mask = _host_consts()
    pl = lambda v: np.ascontiguousarray(np.asarray(v, f).reshape(-1, 128).T)
    lnp = np.stack([pl(inp["ln_emb_g"]), pl(inp["ln_emb_b"])] +
                   sum([[pl(inp["ln_mix_g"][l]), pl(inp["ln_mix_b"][l]), pl(inp["ln_mlp_g"][l]), pl(inp["ln_mlp_b"][l])] for l in range(2)], []), axis=1)
    gateb = np.stack([pl(inp["gate_b"][l]) for l in range(2)], axis=1)
    sinks = np.broadcast_to(np.asarray(inp["attn_sinks"], f)[None], (64, 2, 16)).copy()
    relb = np.concatenate([np.asarray(inp["rel_bias"], f), np.full((1, 16), -60.0, f)], axis=0)

    def gn(a):
        return np.asarray(a, f).reshape(32, 2, 64).transpose(1, 2, 0).reshape(128, 32)

    lam = np.stack([np.stack([gn(inp["ssm_lambda_re"][l]), gn(inp["ssm_lambda_im"][l]),
                              gn(np.broadcast_to(np.asarray(inp["ssm_log_step"][l], f)[:, None], (64, 64)))], axis=1) for l in range(2)], axis=1)

    def gb(a):
        return np.asarray(a, f).reshape(32, 2, 64, 16).transpose(1, 2, 0, 3).reshape(128, 32, 16)

    bmat = np.stack([np.stack([gb(inp["ssm_b_re"][l]), gb(inp["ssm_b_im"][l])], axis=1) for l in range(2)], axis=1)

    def gc(a):
        return np.asarray(a, f).reshape(8, 128, 64).transpose(1, 0, 2)

    cmat = np.stack([np.stack([gc(inp["ssm_c_re"][l]), gc(inp["ssm_c_im"][l])], axis=1) for l in range(2)], axis=1)
    dvec = np.stack([np.asarray(inp["ssm_d"][l], f).reshape(8, 128).T for l in range(2)], axis=1)
    m = {
        "xT": np.ascontiguousarray(np.asarray(inp["x"][b], f).T),
        "metaT": np.ascontiguousarray(np.asarray(inp["meta_tokens"], f).T),
        "lnp": np.ascontiguousarray(lnp), "gateb": np.ascontiguousarray(gateb), "sinks": sinks, "relb": relb,
        "oh": oh, "g2mask": g2mask, "lam": np.ascontiguousarray(lam), "bmat": np.ascontiguousarray(bmat),
        "cmat": np.ascontiguousarray(cmat), "dvec": np.ascontiguousarray(dvec),
    }
    for k in ["in_proj", "ssm_w_glu", "w_attn_up", "w_ssm_up", "w_out", "w_mlp_up", "w_mlp_down"]:
        m[k] = np.ascontiguousarray(np.asarray(inp[k], f))
    return m


def kernel(**inputs):
    nc = build_program()
    shared = None
    in_maps = []
    for core in range(8):
        b = core % 4
        m = _prep_inputs(inputs, b) if shared is None else dict(shared, xT=np.ascontiguousarray(np.asarray(inputs["x"][b], np.float32).T))
        if shared is None:
            shared = m
        in_maps.append(m)
    res = run_bass_kernel_spmd(nc, in_maps, core_ids=list(range(8)))
    out = np.stack([np.ascontiguousarray(res.results[b]["outT"].T) for b in range(4)], axis=0)
    return out.astype(np.float32)
```
